# Optimizing a Trainium2 kernel written in Bass

```python
import jax, jax.numpy as jnp
from jax import lax
import numpy as np

D_MODEL = 1024
BATCH = 8
SEQ = 2048
DEPTH = 2
DEC_BATCH = 128
DEC_SEQ = 4
PAST_LEN = 16384
PAGE_SIZE = 128

N_AB = (DEPTH + 1) // 2
N_C = DEPTH // 2
RMS_EPS = 1e-6
W_A = D_MODEL // 2
H_A = 8
BD_A = W_A // H_A
CONV_W = 4
LRU_C = 8.0
W_B = D_MODEL // 2
HD_B = 64
H_B = W_B // HD_B
LORA_W = 64
LORA_A = 64
LORA_G = 128
GN_EPS = 64e-5
P_A = 2 * W_A
P_B = 3 * W_B + LORA_W + LORA_A + LORA_G
P_AB = P_A + P_B
DK_C = 128
H_C = D_MODEL // DK_C
DV_C = D_MODEL // H_C
F_C = H_C * DK_C
P_C = 2 * F_C + 2 * D_MODEL
CHUNK = 64
D_FF = -(-(8 * D_MODEL) // (3 * 256)) * 256

kernel_name = 'hawk_rwkv7_hgrn2_hybrid_step'

F32 = jnp.float32


def rmsnorm(x, g):
    xf = x.astype(F32)
    y = xf * lax.rsqrt(jnp.mean(xf * xf, axis=-1, keepdims=True) + RMS_EPS)
    return (y * g.astype(F32)).astype(x.dtype)


def causal_conv(u, buf, w, b):
    T = u.shape[1]
    full = jnp.concatenate([buf.astype(u.dtype), u], axis=1)
    out = b + full[:, 0:T] * w[0]
    for j in range(1, CONV_W):
        out = out + full[:, j:j + T] * w[j]
    return out, full[:, full.shape[1] - (CONV_W - 1):]


def _lin_comb(e1, e2):
    a1, b1 = e1
    a2, b2 = e2
    return a1 * a2, a2 * b1 + b2


def rg_lru(u, h0, gr_w, gr_b, gi_w, gi_b, lam):
    B, T, _ = u.shape
    uf = u.astype(F32)
    ub = uf.reshape(B, T, H_A, BD_A)
    r = jax.nn.sigmoid(jnp.einsum('bthi,hij->bthj', ub, gr_w).reshape(B, T, W_A) + gr_b)
    ig = jax.nn.sigmoid(jnp.einsum('bthi,hij->bthj', ub, gi_w).reshape(B, T, W_A) + gi_b)
    log_a = -LRU_C * r * jax.nn.softplus(-lam.astype(F32))
    a = jnp.exp(log_a)
    bterm = jnp.sqrt(-jnp.expm1(2.0 * log_a)) * (ig * uf)
    bterm = bterm.at[:, 0].add(a[:, 0] * h0.astype(F32))
    _, h = lax.associative_scan(_lin_comb, (a, bterm), axis=1)
    return h, h[:, -1]


def _rwkv_step(S, inp):
    r_t, w_t, k_t, v_t, kk_t, a_t = inp
    sk = jnp.einsum('bhvk,bhk->bhv', S, kk_t)
    S = (S * w_t[:, :, None, :] - sk[..., None] * (kk_t * a_t)[:, :, None, :]
         + v_t[..., None] * k_t[:, :, None, :])
    return S, jnp.einsum('bhvk,bhk->bhv', S, r_t)


def rwkv7(pb, prev, S0, mu, w0, w2, a0, a2, g2, k_k, k_a, r_k, lnx_w, lnx_b):
    B, T, _ = pb.shape
    pf = pb.astype(F32)
    shifted = jnp.concatenate([prev[:, None].astype(F32), pf[:, :-1]], axis=1)
    m = pf + (shifted - pf) * mu
    o1 = 3 * W_B
    r = m[..., :W_B]
    k = m[..., W_B:2 * W_B]
    v = m[..., 2 * W_B:o1]
    xw = m[..., o1:o1 + LORA_W]
    xa = m[..., o1 + LORA_W:o1 + LORA_W + LORA_A]
    xg = m[..., o1 + LORA_W + LORA_A:]
    w_log = -jax.nn.softplus(-(w0 + jnp.tanh(xw) @ w2)) - 0.5
    decay = jnp.exp(-jnp.exp(w_log))
    a = jax.nn.sigmoid(a0 + xa @ a2)
    g = jax.nn.sigmoid(xg) @ g2
    hd = lambda z: z.reshape(B, T, H_B, HD_B)
    kk = hd(k * k_k)
    kk = kk * lax.rsqrt(jnp.maximum(jnp.sum(kk * kk, axis=-1, keepdims=True), 1e-24))
    k = k * (1.0 + (a - 1.0) * k_a)
    r4, k4, v4 = hd(r), hd(k), hd(v)
    tm = lambda z: jnp.swapaxes(z, 0, 1)
    S_T, o = lax.scan(_rwkv_step, S0.astype(F32),
                      (tm(r4), tm(hd(decay)), tm(k4), tm(v4), tm(kk), tm(hd(a))))
    o = tm(o)
    mean = jnp.mean(o, axis=-1, keepdims=True)
    var = jnp.mean(jnp.square(o - mean), axis=-1, keepdims=True)
    on = ((o - mean) * lax.rsqrt(var + GN_EPS)).reshape(B, T, W_B) * lnx_w + lnx_b
    bonus = (jnp.sum(r4 * k4 * r_k, axis=-1, keepdims=True) * v4).reshape(B, T, W_B)
    return (on + bonus) * g, S_T, pb[:, -1]


def hgrn2(pc, S0, lb, gn):
    B, T, _ = pc.shape
    pf = pc.astype(F32)
    q = jax.nn.silu(pf[..., :F_C])
    fr = pf[..., F_C:2 * F_C]
    iv = pf[..., 2 * F_C:2 * F_C + D_MODEL]
    gg = pf[..., 2 * F_C + D_MODEL:]
    lbf = lb.astype(F32)
    logf = jnp.logaddexp(jnp.log(lbf), jnp.log1p(-lbf) + jax.nn.log_sigmoid(fr))
    k = (1.0 - lbf) * jax.nn.sigmoid(-fr)
    c = min(CHUNK, T)
    n = -(-T // c)
    pad = n * c - T

    def blocks(z, d):
        z = jnp.pad(z, ((0, 0), (0, pad), (0, 0)))
        return z.reshape(B, n, c, H_C, d).transpose(1, 0, 3, 2, 4)

    mask = jnp.tril(jnp.ones((c, c), bool))[:, :, None]

    def step(S, inp):
        qc, kc, lc, ic = inp
        bc = jnp.cumsum(lc, axis=2)
        diff = bc[:, :, :, None, :] - bc[:, :, None, :, :]
        dec = jnp.exp(jnp.where(mask, diff, -jnp.inf))
        att = jnp.einsum('bhtsk,bhtk,bhsk->bhts', dec, qc, kc)
        o = (jnp.einsum('bhts,bhsv->bhtv', att, ic)
             + jnp.einsum('bhtk,bhkv->bhtv', qc * jnp.exp(bc), S))
        bl = bc[:, :, -1]
        S = (jnp.exp(bl)[..., None] * S
             + jnp.einsum('bhsk,bhsv->bhkv', kc * jnp.exp(bl[:, :, None] - bc), ic))
        return S, o

    S_T, o = lax.scan(step, S0.astype(F32),
                      (blocks(q, DK_C), blocks(k, DK_C), blocks(logf, DK_C), blocks(iv, DV_C)))
    o = o.transpose(1, 0, 3, 2, 4).reshape(B, n * c, H_C * DV_C)[:, :T]
    return rmsnorm(o, gn) * jax.nn.silu(gg), S_T


def trunk(x, conv_st, h_st, shift_st, rS_st, hS_st, P):
    lb_cum = jnp.cumsum(jax.nn.softmax(P['lb_c'].astype(F32), axis=0), axis=0)
    n_conv, n_h, n_shift, n_rS, n_hS = [], [], [], [], []
    for l in range(DEPTH):
        j = l // 2
        xn = rmsnorm(x, P['ln_mix'][l])
        if l % 2 == 0:
            p = xn @ P['w_in_ab'][j]
            u, conv_new = causal_conv(p[..., :W_A], conv_st[j], P['conv_w'][j], P['conv_b'][j])
            h, h_new = rg_lru(u, h_st[j], P['gr_w'][j], P['gr_b'][j], P['gi_w'][j],
                              P['gi_b'][j], P['lru_lambda'][j])
            out_a = jax.nn.gelu(p[..., W_A:P_A].astype(F32)) * h
            out_b, rS_new, shift_new = rwkv7(
                p[..., P_A:], shift_st[j], rS_st[j], P['mu_b'][j], P['w0_b'][j], P['w2_b'][j],
                P['a0_b'][j], P['a2_b'][j], P['g2_b'][j], P['kk_b'][j], P['ka_b'][j],
                P['rk_b'][j], P['lnx_w'][j], P['lnx_b'][j])
            y = jnp.concatenate([out_a, out_b], axis=-1).astype(x.dtype) @ P['w_out_ab'][j]
            n_conv.append(conv_new.astype(conv_st.dtype))
            n_h.append(h_new.astype(h_st.dtype))
            n_shift.append(shift_new.astype(shift_st.dtype))
            n_rS.append(rS_new.astype(rS_st.dtype))
        else:
            pc = xn @ P['w_in_c'][j]
            o, hS_new = hgrn2(pc, hS_st[j], lb_cum[l] - lb_cum[0], P['gn_c'][j])
            y = o.astype(x.dtype) @ P['w_out_c'][j]
            n_hS.append(hS_new.astype(hS_st.dtype))
        x = x + y
        xf = rmsnorm(x, P['ln_ffn'][l])
        x = x + (jax.nn.silu(xf @ P['ffn_gate'][l]) * (xf @ P['ffn_up'][l])) @ P['ffn_down'][l]
    return (rmsnorm(x, P['ln_final']), jnp.stack(n_conv), jnp.stack(n_h), jnp.stack(n_shift),
            jnp.stack(n_rS), jnp.stack(n_hS))


def setup_inputs(seed: int = 0) -> dict:
    key = jax.random.key(seed)
    ks = iter(jax.random.split(key, 48))
    nrm = lambda shape, s: jax.random.normal(next(ks), shape, F32) * s
    uni = lambda shape, lo, hi: jax.random.uniform(next(ks), shape, F32, lo, hi)
    d = {}
    d['x_prompt'] = nrm((BATCH, SEQ, D_MODEL), 1.0)
    d['x_sample'] = nrm((DEC_BATCH, DEC_SEQ, D_MODEL), 1.0)
    d['state_rglru_conv'] = nrm((N_AB, DEC_BATCH, CONV_W - 1, W_A), 1.0)
    d['state_rglru_h'] = nrm((N_AB, DEC_BATCH, W_A), 0.5)
    d['state_rwkv_shift'] = nrm((N_AB, DEC_BATCH, P_B), 1.0)
    d['state_rwkv_S'] = nrm((N_AB, DEC_BATCH, H_B, HD_B, HD_B), 0.5)
    d['state_hgrn_S'] = nrm((N_C, DEC_BATCH, H_C, DK_C, DV_C), 0.5)
    d['ln_mix'] = 1.0 + nrm((DEPTH, D_MODEL), 0.02)
    d['ln_ffn'] = 1.0 + nrm((DEPTH, D_MODEL), 0.02)
    d['ln_final'] = 1.0 + nrm((D_MODEL,), 0.02)
    d['w_in_ab'] = nrm((N_AB, D_MODEL, P_AB), D_MODEL ** -0.5)
    d['conv_w'] = nrm((N_AB, CONV_W, W_A), 0.5)
    d['conv_b'] = nrm((N_AB, W_A), 0.01)
    d['gr_w'] = nrm((N_AB, H_A, BD_A, BD_A), BD_A ** -0.5)
    d['gr_b'] = nrm((N_AB, W_A), 0.01)
    d['gi_w'] = nrm((N_AB, H_A, BD_A, BD_A), BD_A ** -0.5)
    d['gi_b'] = nrm((N_AB, W_A), 0.01)
    s = uni((N_AB, W_A), 0.9, 0.999) ** (1.0 / LRU_C)
    d['lru_lambda'] = jnp.log(s) - jnp.log1p(-s)
    d['mu_b'] = uni((N_AB, P_B), 0.0, 1.0)
    d['w0_b'] = uni((N_AB, W_B), -6.0, -1.0)
    d['w2_b'] = nrm((N_AB, LORA_W, W_B), 0.1 * LORA_W ** -0.5)
    d['a0_b'] = nrm((N_AB, W_B), 0.1)
    d['a2_b'] = nrm((N_AB, LORA_A, W_B), LORA_A ** -0.5)
    d['g2_b'] = nrm((N_AB, LORA_G, W_B), LORA_G ** -0.5)
    d['kk_b'] = 0.85 + nrm((N_AB, W_B), 0.02)
    d['ka_b'] = 1.0 + nrm((N_AB, W_B), 0.02)
    d['rk_b'] = nrm((N_AB, H_B, HD_B), 0.1)
    d['lnx_w'] = 1.0 + nrm((N_AB, W_B), 0.02)
    d['lnx_b'] = nrm((N_AB, W_B), 0.01)
    d['w_out_ab'] = nrm((N_AB, W_A + W_B, D_MODEL), (W_A + W_B) ** -0.5)
    d['w_in_c'] = nrm((N_C, D_MODEL, P_C), D_MODEL ** -0.5)
    d['lb_c'] = nrm((DEPTH, F_C), 1.0)
    d['gn_c'] = 1.0 + nrm((N_C, D_MODEL), 0.02)
    d['w_out_c'] = nrm((N_C, D_MODEL, D_MODEL), D_MODEL ** -0.5)
    d['ffn_gate'] = nrm((DEPTH, D_MODEL, D_FF), D_MODEL ** -0.5)
    d['ffn_up'] = nrm((DEPTH, D_MODEL, D_FF), D_MODEL ** -0.5)
    d['ffn_down'] = nrm((DEPTH, D_FF, D_MODEL), D_FF ** -0.5)
    return d


def reference(x_prompt, x_sample, state_rglru_conv, state_rglru_h, state_rwkv_shift, state_rwkv_S,
              state_hgrn_S, ln_mix, ln_ffn, ln_final, w_in_ab, conv_w, conv_b, gr_w, gr_b, gi_w,
              gi_b, lru_lambda, mu_b, w0_b, w2_b, a0_b, a2_b, g2_b, kk_b, ka_b, rk_b, lnx_w, lnx_b,
              w_out_ab, w_in_c, lb_c, gn_c, w_out_c, ffn_gate, ffn_up, ffn_down):
    P = dict(ln_mix=ln_mix, ln_ffn=ln_ffn, ln_final=ln_final, w_in_ab=w_in_ab, conv_w=conv_w,
             conv_b=conv_b, gr_w=gr_w, gr_b=gr_b, gi_w=gi_w, gi_b=gi_b, lru_lambda=lru_lambda,
             mu_b=mu_b, w0_b=w0_b, w2_b=w2_b, a0_b=a0_b, a2_b=a2_b, g2_b=g2_b, kk_b=kk_b,
             ka_b=ka_b, rk_b=rk_b, lnx_w=lnx_w, lnx_b=lnx_b, w_out_ab=w_out_ab, w_in_c=w_in_c,
             lb_c=lb_c, gn_c=gn_c, w_out_c=w_out_c, ffn_gate=ffn_gate, ffn_up=ffn_up,
             ffn_down=ffn_down)
    bp = x_prompt.shape[0]
    dt = x_prompt.dtype
    z_conv = jnp.zeros((N_AB, bp, CONV_W - 1, W_A), dt)
    z_h = jnp.zeros((N_AB, bp, W_A), dt)
    z_shift = jnp.zeros((N_AB, bp, P_B), dt)
    z_rS = jnp.zeros((N_AB, bp, H_B, HD_B, HD_B), dt)
    z_hS = jnp.zeros((N_C, bp, H_C, DK_C, DV_C), dt)
    y_prompt, p_conv, p_h, p_shift, p_rS, p_hS = trunk(x_prompt, z_conv, z_h, z_shift, z_rS, z_hS, P)
    y_sample, s_conv, s_h, s_shift, s_rS, s_hS = trunk(
        x_sample, state_rglru_conv, state_rglru_h, state_rwkv_shift, state_rwkv_S, state_hgrn_S, P)
    return (y_prompt, y_sample, p_conv, p_h, p_shift, p_rS, p_hS,
            s_conv, s_h, s_shift, s_rS, s_hS)
```

```python
import numpy as np
import concourse.bass as bass
import concourse.mybir as mybir
from concourse.bass_utils import run_bass_kernel_spmd
from concourse.alu_op_type import AluOpType as ALU

F32 = mybir.dt.float32
BF16 = mybir.dt.bfloat16
AF = mybir.ActivationFunctionType
NCORES = 8
import os
LIM = int(os.environ.get('KLIM', '99'))
NBLK = int(os.environ.get('KNBLK', '4'))
KSUB = int(os.environ.get('KSUB', '99'))
FUSE_STAT = int(os.environ.get('KFUSE', '1'))
WCACHE = int(os.environ.get('KWCACHE', '1'))
SAMPLE_LAST = int(os.environ.get('KSLAST', '1'))
PUMPN = int(os.environ.get('KPUMP', '1'))
SAME_SYNC = int(os.environ.get('KSAME', '1'))
KR = int(os.environ.get('KR', '99'))
KA = int(os.environ.get('KA', '99'))
D = 1024
TBP = 512
EW = 0.6065306597126334


class Buf:
    __slots__ = ("name", "w", "r", "dsem", "dcnt")

    def __init__(self, name):
        self.name = name; self.w = None; self.r = {}; self.dsem = None; self.dcnt = 0


class Prog:
    def __init__(self, nc):
        self.nc = nc
        self.engs = {"pe": nc.tensor, "act": nc.scalar, "dve": nc.vector, "pool": nc.gpsimd, "sp": nc.sync}
        self.q = {e: [] for e in self.engs}
        self.esem = {e: nc.alloc_semaphore("es_" + e) for e in self.engs}
        self.ecnt = {e: 0 for e in self.engs}
        self.seen = {e: {} for e in self.engs}
        self.pend = {e: [] for e in self.engs}
        self.dry = False
        self.nbuf = 0
        self.marks = []
        self.pe_ops = []
        self.pe_inc_idx = []

    def _pe_resolve(self, idx):
        import bisect
        k = bisect.bisect_left(self.pe_inc_idx, idx)
        if k < len(self.pe_inc_idx):
            return k + 1
        last = len(self.pe_ops) - 1
        self.pe_ops[last]["inc"] = True
        self.pe_inc_idx.append(last)
        return len(self.pe_inc_idx)

    def buf(self, name=None):
        self.nbuf += 1
        return Buf(name or f"b{self.nbuf}")

    def mark(self, label):
        if not self.dry:
            self.marks.append((label, dict(self.ecnt)))

    def barrier(self):
        if self.dry:
            return
        toks = [(self.esem[e], self.ecnt[e]) for e in ("act", "dve", "pool") if self.ecnt[e] > 0]
        if self.pe_ops:
            toks.append(("PE", len(self.pe_ops) - 1))
        for e in ("pe", "act", "dve", "pool"):
            self.pend[e] = list(toks)

    def _deps(self, e, reads, writes):
        deps = {}

        def add(tok):
            if tok is None:
                return
            s, v = tok
            if s == "PE":
                if e == "pe":
                    return
                s, v = self.esem["pe"], self._pe_resolve(v)
            k = id(s)
            if k not in deps or deps[k][1] < v:
                deps[k] = (s, v)
        for b in reads:
            add(b.w)
        for b in writes:
            add(b.w)
            for t in b.r.values():
                add(t)
        for t in self.pend[e]:
            add(t)
        self.pend[e] = []
        out = []
        seen = self.seen[e]
        for k, (s, v) in deps.items():
            if s is self.esem[e] and (e == "pe" or not SAME_SYNC):
                continue
            if seen.get(k, 0) >= v:
                continue
            seen[k] = v
            out.append((s, v))
        return out

    def _mark(self, tok, reads, writes):
        k = "PE" if tok[0] == "PE" else id(tok[0])
        for b in reads:
            b.r[k] = tok
        for b in writes:
            b.w = tok; b.r = {}

    def op(self, e, fn, reads=(), writes=()):
        if self.dry:
            return
        deps = self._deps(e, reads, writes)
        self.ecnt[e] += 1
        sem = self.esem[e]
        if e == "pe":
            rec = {"inc": False}
            self.pe_ops.append(rec)
            tok = ("PE", len(self.pe_ops) - 1)
            self._mark(tok, reads, writes)

            def run(eng):
                for s, v in deps:
                    eng.wait_ge(s, v)
                ins = fn(eng)
                if rec["inc"]:
                    ins.then_inc(sem, 1)
            self.q[e].append(run)
            return
        tok = (self.esem[e], self.ecnt[e])
        self._mark(tok, reads, writes)

        def run(eng):
            for s, v in deps:
                eng.wait_ge(s, v)
            fn(eng).then_inc(sem, 1)
        self.q[e].append(run)

    def dma(self, e, out, in_, tb, reads=(), writes=(), **kw):
        if self.dry:
            return
        deps = self._deps(e, reads, writes)
        if tb.dsem is None:
            tb.dsem = self.nc.alloc_semaphore("ds_" + tb.name)
        tb.dcnt += 1
        tok = (tb.dsem, 16 * tb.dcnt)
        self._mark(tok, reads, writes)
        dsem = tb.dsem

        def run(eng):
            for s, v in deps:
                eng.wait_ge(s, v)
            eng.dma_start(out=out, in_=in_, **kw).then_inc(dsem, 16)
        self.q[e].append(run)

    def finish(self, out_bufs):
        deps = self._deps("sp", (), out_bufs)

        def run(eng):
            for s, v in deps:
                eng.wait_ge(s, v)
        self.q["sp"].append(run)

    def emit(self):
        nc = self.nc
        with nc.Block() as block:
            @block.tensor
            def _(eng):
                for f in self.q["pe"]:
                    f(eng)

            @block.scalar
            def _(eng):
                for f in self.q["act"]:
                    f(eng)

            @block.vector
            def _(eng):
                for f in self.q["dve"]:
                    f(eng)

            @block.gpsimd
            def _(eng):
                for f in self.q["pool"]:
                    f(eng)

            @block.sync
            def _(eng):
                for f in self.q["sp"]:
                    f(eng)


class T_:
    __slots__ = ("t", "b")

    def __init__(self, t, b):
        self.t = t; self.b = b


def build_program():
    nc = bass.Bass("TRN2", target_bir_lowering=False)
    P = Prog(nc)
    dram_in = {}
    dram_out = {}

    def din(name, shape):
        dram_in[name] = nc.dram_tensor(name, list(shape), F32, kind="ExternalInput").ap()
        return dram_in[name]

    def dout(name, shape):
        dram_out[name] = nc.dram_tensor(name, list(shape), F32, kind="ExternalOutput").ap()
        return dram_out[name]

    xp = din("xp", [2048, D]); xs = din("xs", [64, D])
    st_conv = din("st_conv", [16, 3, 512]); st_h = din("st_h", [16, 512]); st_shift = din("st_shift", [16, 1792])
    st_rS = din("st_rS", [16, 8, 64, 64]); st_hS = din("st_hS", [16, 8, 128, 128])
    w_in_ab = din("w_in_ab", [1024, 2816]); w_out_ab = din("w_out_ab", [1024, 1024])
    w_in_c = din("w_in_c", [1024, 4096]); w_out_c = din("w_out_c", [1024, 1024])
    ffn_gate = [din(f"ffn_gate{l}", [1024, 2816]) for l in range(2)]
    ffn_up = [din(f"ffn_up{l}", [1024, 2816]) for l in range(2)]
    ffn_down = [din(f"ffn_down{l}", [2816, 1024]) for l in range(2)]
    w2d = din("w2", [64, 512]); a2d = din("a2", [64, 512]); g2d = din("g2", [128, 512])
    grwd = din("grw_bd", [128, 512]); giwd = din("giw_bd", [128, 512])
    pk_d = din("pack128", [128, PK_N]); cst_d = din("consts", [128, 256]); cstm_d = din("cmasks", [128, 320])
    yp = dout("yp", [2048, D]); ys = dout("ys", [64, D])
    o_pconv = dout("p_conv", [3, 512]); o_ph = dout("p_h", [1, 512]); o_pshift = dout("p_shift", [1, 1792])
    o_prS = dout("p_rS", [8, 64, 64]); o_phS = dout("p_hS", [8, 128, 128])
    o_sconv = dout("s_conv", [16, 3, 512]); o_sh = dout("s_h", [16, 512]); o_sshift = dout("s_shift", [16, 1792])
    o_srS = dout("s_rS", [16, 8, 64, 64]); o_shS = dout("s_hS", [16, 8, 128, 128])

    class Arena:
        def __init__(self):
            self.off = 16512; self.n = 0; self.hi = 0; self.offs = {}

        def alloc(self, shape, dt, name=None):
            esz = 4 if dt == F32 else 2
            nb = esz
            for s in shape[1:]:
                nb *= s
            off = (self.off + 31) // 32 * 32
            self.off = off + nb
            self.hi = max(self.hi, self.off)
            assert self.off <= 229000, ("SBUF overflow", self.off, name)
            self.n += 1
            nm = f"{name or 't'}_{self.n}"
            t_ = T_(nc.alloc_sbuf_tensor_at(nm, list(shape), dt, offset=off), P.buf(nm))
            self.offs[id(t_)] = off
            return t_

        def alias(self, t_, shape, dt, name="alias"):
            self.n += 1
            return T_(nc.alloc_sbuf_tensor_at(f"{name}_{self.n}", list(shape), dt, offset=self.offs[id(t_)]), t_.b)
    A = Arena()
    al = A.alloc

    PK = al([128, PK_N], F32, "pk"); CST = al([128, 256], F32, "cst"); CSTM = al([128, 320], F32, "cstm")
    xT = al([128, 8, TBP], F32, "xT"); xn = al([128, 8, TBP], BF16, "xn")
    XST = A.alias(xT, [128, 4, D], F32, "xst")
    xtok = al([128, 4, D], F32, "xtok")
    A.n += 1
    OHP = T_(nc.alloc_sbuf_tensor_at(f"ohp_{A.n}", [128, 8, TBP], F32, offset=A.off - 16384), xtok.b)
    NSLOT = 10
    SLOT = 2048
    slots = [al([128, SLOT], BF16, f"slot{i}") for i in range(NSLOT)]
    W2 = al([64, 512], BF16, "W2"); A2 = al([128, 512], BF16, "A2"); G2 = al([128, 512], BF16, "G2")
    GRW = al([128, 512], BF16, "GRW"); GIW = al([128, 512], BF16, "GIW")
    ONESB = al([128, 128], BF16, "onesb"); IDENTB = al([128, 128], BF16, "identb")
    RKM = al([128, 4, 128], F32, "rkm"); BMEAN = al([128, 128], F32, "bmean")
    DER = al([128, 40], F32, "der")
    RM64 = al([128, 2048], BF16, "rm64"); RM4 = al([128, 512], BF16, "rm4")
    PAH = al([128, 4, 3], F32, "pah"); pA_s = al([128, 4, 16, 7], F32, "pA_s")
    H0 = {"p": al([128, 4, 1], F32, "h0p"), "s": al([128, 4, 16], F32, "h0s")}
    SH = {k: {"r": al([128, 4, n], F32), "k": al([128, 4, n], F32), "v": al([128, 4, n], F32),
              "wa": al([128, 1, n], F32), "g": al([128, 1, n], F32)} for k, n in (("p", 1), ("s", 16))}
    PST = al([128, 4, 64], F32, "pst")
    HS = al([128, 8, 128], F32, "hs")
    SIN = [al([64, 4, 128], F32, f"sin{i}") for i in range(2)]
    SOUT = [al([64, 4, 128], F32, f"sout{i}") for i in range(2)]
    HSB = [al([128, 4, 128], F32, f"hsb{i}") for i in range(3)]
    STG = al([64, 512], F32, "stg")
    psb = [T_(nc.alloc_psum_tensor(f"ps{i}", [128, 512], F32), P.buf(f"ps{i}")) for i in range(8)]
    psi = [0]

    PS_RES = [None]

    def ps():
        psi[0] = (psi[0] + 1) % 8
        if psi[0] == PS_RES[0]:
            psi[0] = (psi[0] + 1) % 8
        return psb[psi[0]]

    IDENT = CST.t[:, 0:128]
    BONES = CST.t[:, 128:256]
    class BMask:
        def __init__(self, i):
            self.i = i

        def __getitem__(self, key):
            p_, h_, t_ = key
            base = CSTM.t[p_, self.i * 64 + t_.start:self.i * 64 + t_.stop]
            return base.rearrange("p (o t) -> p o t", o=1).to_broadcast([p_.stop - p_.start, h_.stop - h_.start, t_.stop - t_.start])
    MS, ML, MI, MIN_, IDM = (BMask(i) for i in range(5))
    CB = [CST.b, CSTM.b]

    def pcol(name, j=0, n=1):
        o = PK_OFF[name] + j
        return PK.t[:, o:o + n]

    def pbc(ap2, n):
        return ap2.rearrange("p (m o) -> p m o", o=1).to_broadcast([128, 4, n])

    def bl(xs_):
        return [x.b if isinstance(x, T_) else x for x in xs_]

    def TT(out, a, b, op, R, W, e="dve"):
        P.op(e, lambda en: en.tensor_tensor(out=out, in0=a, in1=b, op=op), bl(R), bl(W))

    def TS(out, a, s1, s2, op0, op1, R, W, e="dve"):
        if op1 is None:
            P.op(e, lambda en: en.tensor_scalar(out=out, in0=a, scalar1=s1, scalar2=None, op0=op0), bl(R), bl(W))
        else:
            P.op(e, lambda en: en.tensor_scalar(out=out, in0=a, scalar1=s1, scalar2=s2, op0=op0, op1=op1), bl(R), bl(W))

    def STT(out, a, s, b, op0, op1, R, W):
        P.op("dve", lambda en: en.scalar_tensor_tensor(out=out, in0=a, scalar=s, in1=b, op0=op0, op1=op1), bl(R), bl(W))

    def ACT(out, a, func, R, W, bias=None, scale=None):
        kw = {}
        if bias is not None:
            kw["bias"] = bias
        if scale is not None:
            kw["scale"] = scale
        P.op("act", lambda en: en.activation(out=out, in_=a, func=func, **kw), bl(R), bl(W))

    def CP(out, a, R, W, e="dve"):
        if e == "act":
            ACT(out, a, AF.Copy, R, W)
        else:
            P.op(e, lambda en: en.tensor_copy(out=out, in_=a), bl(R), bl(W))

    def MSET(ap, val, W, e="dve"):
        P.op(e, lambda en: en.memset(ap, val), [], bl(W))

    def RCP(out, a, R, W):
        P.op("dve", lambda en: en.reciprocal(out=out, in_=a), bl(R), bl(W))

    def MM(out, lhsT, rhs, st, sp_, R, W):
        P.op("pe", lambda en: en.matmul(out, lhsT, rhs, start=st, stop=sp_), bl(R), bl(W))

    def TRX(out, in_, R, W):
        n = in_.shape[0]
        assert in_.shape[1] == 128
        P.op("pe", lambda en: en.transpose(out, in_, IDENT[0:n, 0:n]), bl(R) + CB, bl(W))

    def TR(out, in_, R, W):
        n = in_.shape[0]
        P.op("pe", lambda en: en.matmul(out, in_, IDENT[0:n, 0:n], start=True, stop=True), bl(R) + CB, bl(W))

    def SCAN(out, d0, d1, R, W):
        P.op("dve", lambda en: en.tensor_tensor_scan(out=out, data0=d0, data1=d1, initial=0.0, op0=ALU.mult, op1=ALU.add), bl(R), bl(W))

    def DMA(out, in_, tb, R=(), W=(), e="sp", **kw):
        P.dma(e, out, in_, tb.b if isinstance(tb, T_) else tb, bl(R), bl(W), **kw)

    class WStream:
        def __init__(self):
            self.units = []; self.i = 0; self.issued = 0

        def reset(self):
            self.i = 0; self.issued = 0
            self.off = {}; tot = 0
            for u in self.units:
                key = (id(u[0]),) + tuple(u[1:])
                if key not in self.off:
                    self.off[key] = tot; tot += u[3] * u[5]
            self.scr = nc.dram_tensor("wscr", [128, tot], BF16).ap()
            self.sbufs = {k: P.buf("scr") for k in self.off}
            self.done = set()
            self.wb = [P.buf(f"wb{i}") for i in range(NSLOT)]

        def _issue(self, j):
            wd, r0, pk, kc, c0, ncol = self.units[j]
            key = (id(wd), r0, pk, kc, c0, ncol)
            s = slots[j % NSLOT]
            n_ = kc * ncol; off = self.off[key]
            if WCACHE and key in self.done:
                DMA(s.t[0:pk, 0:n_], self.scr[0:pk, off:off + n_], s, R=[self.sbufs[key]], W=[s], e="pool")
                return
            dst = s.t[0:pk, 0:n_].rearrange("p (k n) -> p k n", k=kc)
            src = wd[r0:r0 + kc * pk, c0:c0 + ncol].rearrange("(k p) n -> p k n", p=pk)
            DMA(dst, src, s, W=[s], e="pool")
            if WCACHE:
                DMA(self.scr[0:pk, off:off + n_], s.t[0:pk, 0:n_], self.wb[j % NSLOT], R=[s], W=[self.sbufs[key]], e="sp")
                self.done.add(key)

        def get(self, wd, r0, pk, kc, c0, ncol):
            assert kc * ncol <= SLOT
            if P.dry:
                self.units.append((wd, r0, pk, kc, c0, ncol))
                return None, None
            j = self.i
            assert self.units[j][1:] == (r0, pk, kc, c0, ncol)
            self.i += 1
            lim = min(len(self.units), j + NSLOT - 2)
            while self.issued < lim:
                self._issue(self.issued); self.issued += 1
            s = slots[j % NSLOT]
            return s.t[0:pk, 0:kc * ncol].rearrange("p (k n) -> p k n", k=kc), s
    WSTR = WStream()

    def setup():
        DMA(PK.t[:], pk_d, PK, W=[PK]); DMA(CST.t[:], cst_d, CST, W=[CST]); DMA(CSTM.t[:], cstm_d, CSTM, W=[CSTM])
        DMA(W2.t[:], w2d, W2, W=[W2], e="pool"); DMA(A2.t[64:128, :], a2d, A2, W=[A2], e="pool")
        DMA(G2.t[:], g2d, G2, W=[G2], e="pool"); DMA(GRW.t[:], grwd, GRW, W=[GRW], e="pool")
        DMA(GIW.t[:], giwd, GIW, W=[GIW], e="pool")
        MSET(ONESB.t[:], 1.0, [ONESB])
        CP(IDENTB.t[:], IDENT, CB, [IDENTB])
        MSET(RM64.t[:], 1.0, [RM64]); MSET(RM4.t[:], 1.0, [RM4])
        MSET(RM64.t[:].rearrange("p (j c) -> p j c", c=64)[:, :, 0:1], 0.0, [RM64])
        MSET(RM4.t[:].rearrange("p (j c) -> p j c", c=4)[:, :, 0:1], 0.0, [RM4])
        TS(BMEAN.t[:], BONES, 1.0 / 64.0, None, ALU.mult, None, CB, [BMEAN])
        for m in range(4):
            TS(RKM.t[:, m, :], BONES, pcol("rk", m), None, ALU.mult, None, CB + [PK], [RKM])
        ACT(DER.t[:, 0:4], pcol("lam", 0, 4), AF.Exp, [PK], [DER], scale=-1.0)
        ACT(DER.t[:, 0:4], DER.t[:, 0:4], AF.Ln, [DER], [DER], bias=1.0)
        TS(DER.t[:, 4:8], DER.t[:, 0:4], -16.0, None, ALU.mult, None, [DER], [DER])
        TS(DER.t[:, 0:4], DER.t[:, 0:4], -8.0, None, ALU.mult, None, [DER], [DER])
        TT(DER.t[:, 8:16], pcol("lb1", 0, 8), pcol("lb0", 0, 8), ALU.subtract, [PK], [DER])
        ACT(DER.t[:, 8:16], DER.t[:, 8:16], AF.Sigmoid, [DER], [DER])
        TS(DER.t[:, 16:24], DER.t[:, 8:16], -1.0, 1.0, ALU.mult, ALU.add, [DER], [DER])
        TS(DER.t[:, 24:28], pcol("ka", 0, 4), -1.0, 1.0, ALU.mult, ALU.add, [PK], [DER])
        for t in (PAH, H0["p"], PST, HS) + tuple(SH["p"].values()):
            MSET(t.t[:], 0.0, [t])

    NSP = lambda m: DER.t[:, m:m + 1]
    NSP2 = lambda m: DER.t[:, 4 + m:5 + m]
    LB = lambda h: DER.t[:, 8 + h:9 + h]
    OML = lambda h: DER.t[:, 16 + h:17 + h]
    OMKA = lambda m: DER.t[:, 24 + m:25 + m]

    def rmsnorm(src, gname, dst, TB, mark):
        A.off = mark
        SQ = al([128, 8, TB], BF16, "sq"); RS = al([128, TB], F32, "rs")
        ACT(SQ.t[:], src.t[:, :, 0:TB], AF.Square, [src], [SQ])
        bk = ps()
        for k in range(8):
            MM(bk.t[:, 0:TB], ONESB.t[:], SQ.t[:, k, :], k == 0, k == 7, [ONESB, SQ], [bk])
        ACT(RS.t[:], bk.t[:, 0:TB], AF.Sqrt, [bk], [RS], bias=1e-6, scale=1.0 / 1024.0)
        RCP(RS.t[:], RS.t[:], [RS], [RS])
        for k in range(8):
            STT(dst.t[:, k, 0:TB], src.t[:, k, 0:TB], pcol(gname, k), RS.t[:], ALU.mult, ALU.mult, [src, RS, PK], [dst])
        P.barrier()

    def lin(wd, r0, pk, kc, c0, ncol, rhs, rbufs, N, epi, mch=128):
        if kc > 11:
            assert ncol == 128 and kc % 2 == 0
            kh = kc // 2
            s0, b0 = WSTR.get(wd, r0, pk, kh, c0, ncol)
            s1, b1 = WSTR.get(wd, r0 + kh * pk, pk, kh, c0, ncol)
            if P.dry:
                return
            bk = ps()
            for k in range(kc):
                sl_, sb_ = (s0, b0) if k < kh else (s1, b1)
                MM(bk.t[0:mch, 0:N], sl_[:, k % kh, 0:mch], rhs(k), k == 0, k == kc - 1, [sb_] + rbufs, [bk])
            epi(0, bk)
            return
        uw = 256
        for u in range((ncol + uw - 1) // uw):
            w_ = min(uw, ncol - u * uw)
            slot, sb = WSTR.get(wd, r0, pk, kc, c0 + u * uw, w_)
            if P.dry:
                continue
            for m in range(w_ // mch):
                bk = ps()
                for k in range(kc):
                    MM(bk.t[0:mch, 0:N], slot[:, k, m * mch:(m + 1) * mch], rhs(k), k == 0, k == kc - 1, [sb] + rbufs, [bk])
                epi(u * (uw // mch) + m, bk)

    def run_block(kind, bi):
        if kind == "p":
            nseq, L, TB, c, TBR = 1, 512, 512, 64, 128
            xd = xp[bi * 512:(bi + 1) * 512, :]; yd = yp[bi * 512:(bi + 1) * 512, :]
            pA = None
        else:
            nseq, L, TB, c, TBR = 16, 4, 64, 4, 64
            xd = xs; yd = ys
            pA = pA_s
        nlev = {64: 5, 4: 1}[c]
        RM = RM64 if c == 64 else RM4
        h0 = H0[kind]; sh = SH[kind]
        last = (kind == "p" and bi == 3)
        base = A.off
        P.barrier()

        if kind == "p" and bi == 0:
            MSET(PST.t[:], 0.0, [PST])
        P.mark(f"{kind}{bi}:load")
        ntt = max(1, TB // 128); rows = min(128, TB)
        if kind == "p":
            DMA(xtok.t[:], xd.rearrange("(j p) d -> p j d", p=128), xtok, W=[xtok])
        else:
            DMA(xtok.t[0:64, 0, :], xd, xtok, W=[xtok])
        for j in range(ntt):
            for g in range(2):
                bk = ps()
                for q in range(4):
                    TRX(bk.t[:, q * 128:q * 128 + rows], xtok.t[0:rows, j, (g * 4 + q) * 128:(g * 4 + q + 1) * 128], [xtok], [bk])
                CP(xT.t[:, g * 4:g * 4 + 4, j * 128:j * 128 + rows],
                   bk.t[:].rearrange("p (q t) -> p q t", q=4)[:, :, 0:rows], [bk], [xT], e="act")

        if kind == "s":
            DMA(STG.t[0:48, 0:512], st_conv.rearrange("s j f -> (s j) f"), STG, W=[STG])
            bk = ps()
            for m in range(4):
                TR(bk.t[:, m * 48:(m + 1) * 48], STG.t[0:48, m * 128:(m + 1) * 128], [STG], [bk])
            for m in range(4):
                CP(pA.t[:, m, :, 0:3], bk.t[:, m * 48:(m + 1) * 48].rearrange("p (s j) -> p s j", j=3), [bk], [pA])
            DMA(STG.t[0:16, 0:512], st_h, STG, W=[STG])
            bk = ps()
            for m in range(4):
                TR(bk.t[:, m * 16:(m + 1) * 16], STG.t[0:16, m * 128:(m + 1) * 128], [STG], [bk])
            CP(h0.t[:], bk.t[:, 0:64].rearrange("p (m s) -> p m s", m=4), [bk], [h0])
            bk = ps()
            for g0 in range(0, 14, 4):
                w_ = min(4, 14 - g0)
                DMA(STG.t[0:16, 0:w_ * 128], st_shift[:, g0 * 128:(g0 + w_) * 128], STG, W=[STG])
                for q in range(w_):
                    TR(bk.t[:, (g0 + q) * 16:(g0 + q + 1) * 16], STG.t[0:16, q * 128:(q + 1) * 128], [STG], [bk])
            for ii, nm in enumerate(("r", "k", "v")):
                CP(sh[nm].t[:], bk.t[:, ii * 64:(ii + 1) * 64].rearrange("p (m s) -> p m s", m=4), [bk], [sh[nm]])
            CP(sh["wa"].t[:, 0, :], bk.t[:, 192:208], [bk], [sh["wa"]])
            CP(sh["g"].t[:, 0, :], bk.t[:, 208:224], [bk], [sh["g"]])

        if LIM <= 0:
            A.off = base
            return
        P.mark(f"{kind}{bi}:L0norm+rglru")
        rmsnorm(xT, "ln_mix0", xn, TB, base)
        A.off = base
        OA = al([128, 4, TB], BF16, "OA"); OB = al([128, 4, TB], BF16, "OB")
        m_l0 = A.off
        if kind == "p":
            pA = al([128, 4, 1, 515], F32, "pA")
            CP(pA.t[:, :, 0, 0:3], PAH.t[:], [PAH], [pA])
        GA = al([128, 4, TB], F32, "GA")
        U4 = [al([128, TB], F32, "U") for _i in range(4)]; UB4 = [al([128, TB], BF16, "UB") for _i in range(4)]
        R4 = [al([128, TB], F32, "R_") for _i in range(4)]; IG4 = [al([128, TB], F32, "IG") for _i in range(4)]
        AA4 = [al([128, TB], F32, "AA_") for _i in range(4)]; S4 = [al([128, TB], F32, "S_") for _i in range(4)]
        HH4 = [al([128, TB], F32, "HH_") for _i in range(4)]; TM4 = [al([128, 16], F32, "tmps") for _i in range(4)]
        U = U4[0]; TMPS = TM4[0]
        xr = lambda k: xn.t[:, k, 0:TB]

        def epi_x(m, bk):
            CP(pA.t[:, m, :, 3:3 + L], bk.t[:, 0:TB].rearrange("p (s l) -> p s l", l=L), [bk], [pA], e="act")
        if KSUB <= 0:
            A.off = base
            return
        lin(w_in_ab, 0, 128, 8, 0, 512, xr, [xn], TB, epi_x)
        if KSUB <= 1:
            A.off = base
            return

        def epi_g(m, bk):
            ACT(U.t[:], bk.t[:, 0:TB], AF.Square, [bk], [U])
            TS(U.t[:], U.t[:], 0.044715, 1.0, ALU.mult, ALU.add, [U], [U])
            TT(U.t[:], U.t[:], bk.t[:, 0:TB], ALU.mult, [U, bk], [U])
            ACT(U.t[:], U.t[:], AF.Sigmoid, [U], [U], scale=1.5957691216057308)
            TT(GA.t[:, m, :], U.t[:], bk.t[:, 0:TB], ALU.mult, [U, bk], [GA])
        lin(w_in_ab, 0, 128, 8, 512, 512, xr, [xn], TB, epi_g)
        if KSUB <= 2:
            A.off = base
            return
        if not P.dry:
            v3 = lambda t: t.t[:].rearrange("p (s l) -> p s l", l=L)
            M4 = range(4)
            for m in M4:
                TS(v3(U4[m]), pA.t[:, m, :, 0:L], pcol("conv_w", m), pcol("conv_b", m), ALU.mult, ALU.add, [pA, PK], [U4[m]])
            for j in range(1, 4):
                for m in M4:
                    STT(v3(U4[m]), pA.t[:, m, :, j:j + L], pcol("conv_w", 4 * j + m), v3(U4[m]), ALU.mult, ALU.add, [pA, PK, U4[m]], [U4[m]])
            for m in M4:
                CP(UB4[m].t[:], U4[m].t[:], [U4[m]], [UB4[m]], e="act")
            for m in M4:
                bk = ps()
                MM(bk.t[:, 0:TB], GRW.t[:, m * 128:(m + 1) * 128], UB4[m].t[:], True, True, [GRW, UB4[m]], [bk])
                ACT(R4[m].t[:], bk.t[:, 0:TB], AF.Sigmoid, [bk, PK], [R4[m]], bias=pcol("gr_b", m))
            for m in M4:
                bk = ps()
                MM(bk.t[:, 0:TB], GIW.t[:, m * 128:(m + 1) * 128], UB4[m].t[:], True, True, [GIW, UB4[m]], [bk])
                ACT(IG4[m].t[:], bk.t[:, 0:TB], AF.Sigmoid, [bk, PK], [IG4[m]], bias=pcol("gi_b", m))
            for m in M4:
                TT(IG4[m].t[:], IG4[m].t[:], U4[m].t[:], ALU.mult, [IG4[m], U4[m]], [IG4[m]])
            for m in M4:
                ACT(AA4[m].t[:], R4[m].t[:], AF.Exp, [R4[m], DER], [AA4[m]], scale=NSP(m))
            for m in M4:
                ACT(S4[m].t[:], R4[m].t[:], AF.Exp, [R4[m], DER], [S4[m]], scale=NSP2(m))
            for m in M4:
                ACT(S4[m].t[:], S4[m].t[:], AF.Sqrt, [S4[m]], [S4[m]], bias=1.0, scale=-1.0)
            for m in M4:
                TT(TM4[m].t[:, 0:nseq], v3(AA4[m])[:, :, 0], h0.t[:, m, :], ALU.mult, [AA4[m], h0], [TM4[m]])
            for m in M4:
                MSET(v3(AA4[m])[:, :, 0:1], 0.0, [AA4[m]])
            for m in M4:
                TT(IG4[m].t[:], IG4[m].t[:], S4[m].t[:], ALU.mult, [IG4[m], S4[m]], [IG4[m]])
            for m in M4:
                TT(v3(IG4[m])[:, :, 0], v3(IG4[m])[:, :, 0], TM4[m].t[:, 0:nseq], ALU.add, [IG4[m], TM4[m]], [IG4[m]])
            for m in M4:
                SCAN(HH4[m].t[:], AA4[m].t[:], IG4[m].t[:], [AA4[m], IG4[m]], [HH4[m]])
            for m in M4:
                CP(h0.t[:, m, :], v3(HH4[m])[:, :, L - 1], [HH4[m]], [h0])
            for m in M4:
                TT(OA.t[:, m, :], GA.t[:, m, :], HH4[m].t[:], ALU.mult, [GA, HH4[m]], [OA])
        if KSUB <= 3:
            A.off = base
            return
        if not P.dry:
            if kind == "s" or last:
                ns3 = nseq * 3
                SG0 = al([128, 4, ns3], F32, "cstage")
                CP(SG0.t[:].rearrange("p m (s j) -> p m s j", j=3), pA.t[:, :, :, L:L + 3], [pA], [SG0])
                bk = ps()
                for m in range(4):
                    TR(bk.t[0:ns3, m * 128:(m + 1) * 128], SG0.t[:, m, :], [SG0], [bk])
                CP(STG.t[0:ns3, 0:512], bk.t[0:ns3, 0:512], [bk], [STG])
                od = o_sconv.rearrange("s j f -> (s j) f") if kind == "s" else o_pconv
                DMA(od, STG.t[0:ns3, 0:512], STG, R=[STG])
                bk = ps()
                for m in range(4):
                    TR(bk.t[0:nseq, m * 128:(m + 1) * 128], h0.t[:, m, :], [h0], [bk])
                CP(STG.t[0:nseq, 0:512], bk.t[0:nseq, 0:512], [bk], [STG])
                od = o_sh if kind == "s" else o_ph
                DMA(od, STG.t[0:nseq, 0:512], STG, R=[STG])
            else:
                CP(PAH.t[:], pA.t[:, :, 0, L:L + 3], [pA], [PAH])

        if LIM <= 1:
            A.off = base
            return
        P.mark(f"{kind}{bi}:rwkv")
        A.off = m_l0
        P.barrier()
        nsub = TB // TBR
        ns = nseq if kind == "s" else 1
        Ls = L if kind == "s" else TBR
        nch = TBR // c
        gch = 2 if kind == "p" else 1
        PB = {nm: al([128, 4 * ns, 1 + Ls], F32, "PB" + nm) for nm in ("r", "k", "v")}
        PBWA = al([128, ns, 1 + Ls], F32, "PBWA"); PBG = al([128, ns, 1 + Ls], F32, "PBG")
        X = {nm: al([128, 4, TBR], F32, "X" + nm) for nm in ("r", "k", "v")}
        DD = al([128, 4, TBR], F32, "DD"); AA = al([128, 4, TBR], F32, "AA"); KK = al([128, 4, TBR], F32, "KK")
        SG = al([128, 4, TBR], F32, "SG"); CSG = al([128, 4, TBR], F32, "CSG"); TG = al([128, 4, TBR], F32, "TG")
        TWA = al([128, TBR], BF16, "TWA"); SXG = al([128, TBR], BF16, "SXG"); XWA = al([128, TBR], F32, "XWA")
        RS2 = al([128, TBR], F32, "RS2")
        SETS = []
        for _i in range(min(3, nsub)):
            SETS.append({nm: al([128, 4, TBR], BF16, nm) for nm in ("KAPG", "RG", "KDIV", "BDIV", "VB")})
            SETS[-1].update({"BON": al([128, 4, TBR], F32, "BON"), "GT": al([128, 4, TBR], F32, "GT"), "GC": al([128, 4, nch], F32, "GC")})
        OO = al([128, 4, TBR], F32, "OO"); TGP = al([128, 4, TBR], F32, "TGP"); RSP = al([128, TBR], F32, "RSP")
        CTS = [{nm: al([128, gch, 4, 64], BF16, nm) for nm in ("VT", "KDT", "BDT", "AKKT", "ARKT", "ARBT", "M_", "MT", "RT")}
               for _i in range(min(2, nsub))]
        WS_ = al([128, 4, 64], BF16, "WS_"); US = al([128, 4, 64], BF16, "US"); PB16 = al([128, 4, 64], BF16, "PB16")
        PBL = [P.buf("pb16_l0"), P.buf("pb16_l1")]
        LW = [P.buf("ws_l0"), P.buf("ws_l1")]; LU = [P.buf("us_l0"), P.buf("us_l1")]

        def rw_prep(sub):
            S = SETS[sub % len(SETS)]
            KAPG, RG, KDIV, BDIV, VB, BON, GT, GC = (S[k_] for k_ in ("KAPG", "RG", "KDIV", "BDIV", "VB", "BON", "GT", "GC"))
            n0 = sub * TBR
            sq0 = 0
            xrs = lambda k: xn.t[:, k, n0:n0 + TBR]

            def hist(pb, shx, nt):
                CP(pb.t[:].rearrange("p (m s) l -> p m s l", m=nt)[:, :, :, 0], shx.t[:, :, sq0:sq0 + ns], [shx], [pb])
            if not P.dry:
                for nm in ("r", "k", "v"):
                    hist(PB[nm], sh[nm], 4)
                hist(PBWA, sh["wa"], 1); hist(PBG, sh["g"], 1)

            def mk_epi(pb, nt):
                def epi(m, bk):
                    CP(pb.t[:].rearrange("p (m s) l -> p m s l", m=nt)[:, m, :, 1:1 + Ls],
                       bk.t[:, 0:TBR].rearrange("p (s l) -> p s l", l=Ls), [bk], [pb], e="act")
                return epi
            lin(w_in_ab, 0, 128, 8, 1024, 256, xrs, [xn], TBR, mk_epi(PB["r"], 4)); yield
            lin(w_in_ab, 0, 128, 8, 1280, 256, xrs, [xn], TBR, lambda m, bk: mk_epi(PB["r"], 4)(m + 2, bk)); yield
            lin(w_in_ab, 0, 128, 8, 1536, 256, xrs, [xn], TBR, mk_epi(PB["k"], 4)); yield
            lin(w_in_ab, 0, 128, 8, 1792, 256, xrs, [xn], TBR, lambda m, bk: mk_epi(PB["k"], 4)(m + 2, bk)); yield
            lin(w_in_ab, 0, 128, 8, 2048, 256, xrs, [xn], TBR, mk_epi(PB["v"], 4)); yield
            lin(w_in_ab, 0, 128, 8, 2304, 256, xrs, [xn], TBR, lambda m, bk: mk_epi(PB["v"], 4)(m + 2, bk)); yield

            def epi_l(m, bk):
                mk_epi(PBWA if m == 0 else PBG, 1)(0, bk)
            lin(w_in_ab, 0, 128, 8, 2560, 256, xrs, [xn], TBR, epi_l); yield
            if P.dry:
                return

            def save(pb, shx, nt):
                CP(shx.t[:, :, sq0:sq0 + ns], pb.t[:].rearrange("p (m s) l -> p m s l", m=nt)[:, :, :, Ls], [pb], [shx])
            for nm in ("r", "k", "v"):
                save(PB[nm], sh[nm], 4)
            save(PBWA, sh["wa"], 1); save(PBG, sh["g"], 1)
            yield
            v3s = lambda ap: ap.rearrange("p (s l) -> p s l", l=Ls)
            for nm in ("r", "k", "v"):
                pb = PB[nm]
                TT(DD.t[:].rearrange("p m (s l) -> p (m s) l", l=Ls), pb.t[:, :, 0:Ls], pb.t[:, :, 1:1 + Ls], ALU.subtract, [pb], [DD])
                TT(DD.t[:], DD.t[:], pbc(pcol("mu_" + nm, 0, 4), TBR), ALU.mult, [DD, PK], [DD])
                TT(X[nm].t[:].rearrange("p m (s l) -> p (m s) l", l=Ls), DD.t[:].rearrange("p m (s l) -> p (m s) l", l=Ls), pb.t[:, :, 1:1 + Ls], ALU.add, [DD, pb], [X[nm]])
                yield
            for pb, mu, dst in ((PBWA, "mu_wa", XWA), (PBG, "mu_g", RS2)):
                TT(v3s(DD.t[:, 0, :]), pb.t[:, :, 0:Ls], pb.t[:, :, 1:1 + Ls], ALU.subtract, [pb], [DD])
                STT(v3s(dst.t[:]), v3s(DD.t[:, 0, :]), pcol(mu), pb.t[:, :, 1:1 + Ls], ALU.mult, ALU.add, [DD, pb, PK], [dst])
            ACT(TWA.t[0:64, :], XWA.t[0:64, :], AF.Tanh, [XWA], [TWA])
            ACT(TWA.t[64:128, :], XWA.t[64:128, :], AF.Copy, [XWA], [TWA])
            ACT(SXG.t[:], RS2.t[:], AF.Sigmoid, [RS2], [SXG])
            yield
            b3v = lambda bk_: bk_.t[:, 0:4 * TBR].rearrange("p (m t) -> p m t", m=4)
            bk = ps()
            for m in range(4):
                MM(bk.t[:, m * TBR:(m + 1) * TBR], W2.t[0:64, m * 128:(m + 1) * 128], TWA.t[0:64, :], True, True, [W2, TWA], [bk])
            TT(SG.t[:], b3v(bk), pbc(pcol("w0", 0, 4), TBR), ALU.add, [bk, PK], [SG])
            ACT(SG.t[:], SG.t[:], AF.Sigmoid, [SG], [SG])
            yield
            bk = ps()
            for m in range(4):
                MM(bk.t[:, m * TBR:(m + 1) * TBR], A2.t[64:128, m * 128:(m + 1) * 128], TWA.t[64:128, :], True, True, [A2, TWA], [bk])
            TT(AA.t[:], b3v(bk), pbc(pcol("a0", 0, 4), TBR), ALU.add, [bk, PK], [AA])
            ACT(AA.t[:], AA.t[:], AF.Sigmoid, [AA], [AA])
            yield
            bk = ps()
            for m in range(4):
                MM(bk.t[:, m * TBR:(m + 1) * TBR], G2.t[:, m * 128:(m + 1) * 128], SXG.t[:], True, True, [G2, SXG], [bk])
            CP(GT.t[:], b3v(bk), [bk], [GT], e="act")
            yield
            fl = lambda t: t.t[:].rearrange("p m t -> p (m t)")
            SCAN(fl(CSG), RM.t[:, 0:4 * TBR], fl(SG), [RM, SG], [CSG])
            TT(KK.t[:], X["k"].t[:], pbc(pcol("kkb", 0, 4), TBR), ALU.mult, [X["k"], PK], [KK])
            TT(DD.t[:], KK.t[:], KK.t[:], ALU.mult, [KK], [DD])
            yield
            bk = ps()
            for m in range(4):
                MM(bk.t[:, m * TBR:(m + 1) * TBR], BONES, DD.t[:, m, :], True, True, CB + [DD], [bk])
            TS(DD.t[:], b3v(bk), 1e-24, None, ALU.max, None, [bk], [DD])
            ACT(DD.t[:], DD.t[:], AF.Sqrt, [DD], [DD])
            RCP(DD.t[:], DD.t[:], [DD], [DD])
            TT(KK.t[:], KK.t[:], DD.t[:], ALU.mult, [KK, DD], [KK])
            yield
            TT(DD.t[:], AA.t[:], pbc(pcol("ka", 0, 4), TBR), ALU.mult, [AA, PK], [DD])
            TT(DD.t[:], DD.t[:], pbc(DER.t[:, 24:28], TBR), ALU.add, [DD, DER], [DD])
            TT(X["k"].t[:], X["k"].t[:], DD.t[:], ALU.mult, [X["k"], DD], [X["k"]])
            TT(DD.t[:], X["r"].t[:], X["k"].t[:], ALU.mult, [X["r"], X["k"]], [DD])
            yield
            bk = ps()
            for m in range(4):
                MM(bk.t[:, m * TBR:(m + 1) * TBR], RKM.t[:, m, :], DD.t[:, m, :], True, True, [RKM, DD], [bk])
            TT(BON.t[:], b3v(bk), X["v"].t[:], ALU.mult, [bk, X["v"]], [BON])
            CP(VB.t[:], X["v"].t[:], [X["v"]], [VB], e="act")
            yield
            TT(DD.t[:], KK.t[:], AA.t[:], ALU.mult, [KK, AA], [DD])
            ACT(TG.t[:], CSG.t[:], AF.Exp, [CSG], [TG], scale=-EW)
            TT(RG.t[:], X["r"].t[:], TG.t[:], ALU.mult, [X["r"], TG], [RG])
            CP(GC.t[:], TG.t[:].rearrange("p m (j c) -> p m j c", c=c)[:, :, :, c - 1], [TG], [GC])
            yield
            ACT(TG.t[:], CSG.t[:], AF.Exp, [CSG], [TG], scale=EW)
            TT(KDIV.t[:], X["k"].t[:], TG.t[:], ALU.mult, [X["k"], TG], [KDIV])
            TT(BDIV.t[:], DD.t[:], TG.t[:], ALU.mult, [DD, TG], [BDIV])
            yield
            TT(CSG.t[:], CSG.t[:], SG.t[:], ALU.subtract, [CSG, SG], [CSG])
            ACT(TG.t[:], CSG.t[:], AF.Exp, [CSG], [TG], scale=-EW)
            TT(KAPG.t[:], KK.t[:], TG.t[:], ALU.mult, [KK, TG], [KAPG])
            yield

        def pump(gen, n=1):
            if gen is None:
                return
            for _ in range(n):
                try:
                    next(gen)
                except StopIteration:
                    return

        def rw_an(sub, gi_):
            S = SETS[sub % len(SETS)]
            KAPG, RG, KDIV, BDIV, VB, BON, GT, GC = (S[k_] for k_ in ("KAPG", "RG", "KDIV", "BDIV", "VB", "BON", "GT", "GC"))
            C_ = CTS[sub % len(CTS)]
            VT, KDT, BDT, AKKT, ARKT, ARBT, M_, MT, RT = (C_[k_] for k_ in ("VT", "KDT", "BDT", "AKKT", "ARKT", "ARBT", "M_", "MT", "RT"))
            n0 = sub * TBR
            sq0 = 0
            for j in range(gi_ * gch, (gi_ + 1) * gch):
                jl = j % gch
                cs = slice(j * c, (j + 1) * c)
                for par in range(2):
                    P_ = slice(par * 64, par * 64 + 64); Pc = slice(par * 64, par * 64 + c)
                    for src, dst, neg in ((VB, VT, False), (KDIV, KDT, False), (BDIV, BDT, True)):
                        bk = ps()
                        for i in range(4):
                            MM(bk.t[Pc, i * 64:(i + 1) * 64], src.t[P_, i, cs], IDENTB.t[P_, P_], True, True, [src, IDENTB], [bk])
                        o_ = dst.t[Pc, jl, :, :].rearrange("p m f -> p (m f)")
                        if neg:
                            ACT(o_, bk.t[Pc, 0:256], AF.Copy, [bk], [dst], scale=-1.0)
                        else:
                            ACT(o_, bk.t[Pc, 0:256], AF.Copy, [bk], [dst])
                    yield
                    specs = ((KDIV, KAPG, AKKT, MS), (BDIV, KAPG, MT, MS), (KAPG, BDIV, M_, ML), (KDIV, RG, ARKT, MI), (BDIV, RG, ARBT, MIN_))
                    for lt, rt, dst, msk in specs:
                        bk = ps()
                        for i in range(4):
                            MM(bk.t[Pc, i * c:(i + 1) * c], lt.t[P_, i, cs], rt.t[P_, i, cs], True, True, [lt, rt], [bk])
                        TT(dst.t[Pc, jl, :, 0:c], bk.t[Pc, 0:4 * c].rearrange("p (h t) -> p h t", h=4), msk[Pc, slice(0, 4), slice(0, c)], ALU.mult, [bk] + CB, [dst])
                    TT(RT.t[Pc, jl, :, 0:c], IDM[Pc, slice(0, 4), slice(0, c)], MT.t[Pc, jl, :, 0:c], ALU.subtract, CB + [MT], [RT])
                    yield
            for lv in range(nlev if KR > 1 else 0):
                for j in range(gi_ * gch, (gi_ + 1) * gch):
                    jl = j % gch
                    for par in range(2):
                        Pc = slice(par * 64, par * 64 + c)
                        v4 = lambda b_: b_.t[Pc, 0:4 * c].rearrange("p (h t) -> p h t", h=4)
                        bka = ps()
                        for i in range(4):
                            MM(bka.t[Pc, i * c:(i + 1) * c], MT.t[Pc, jl, i, 0:c], M_.t[Pc, jl, i, 0:c], True, True, [MT, M_], [bka])
                        if lv < nlev - 1:
                            bkb = ps()
                            for i in range(4):
                                MM(bkb.t[Pc, i * c:(i + 1) * c], M_.t[Pc, jl, i, 0:c], MT.t[Pc, jl, i, 0:c], True, True, [MT, M_], [bkb])
                        CP(M_.t[Pc, jl, :, 0:c], v4(bka), [bka], [M_], e="act")
                        if lv < nlev - 1:
                            CP(MT.t[Pc, jl, :, 0:c], v4(bkb), [bkb], [MT], e="act")
                        bkc = ps()
                        for i in range(4):
                            MM(bkc.t[Pc, i * c:(i + 1) * c], IDENTB.t[Pc, Pc], RT.t[Pc, jl, i, 0:c], True, False, [IDENTB, RT], [bkc])
                            MM(bkc.t[Pc, i * c:(i + 1) * c], M_.t[Pc, jl, i, 0:c], RT.t[Pc, jl, i, 0:c], False, True, [M_, RT], [bkc])
                        if lv % 2 == 0:
                            CP(RT.t[Pc, jl, :, 0:c], v4(bkc), [bkc], [RT], e="act")
                        else:
                            CP(RT.t[Pc, jl, :, 0:c], v4(bkc), [bkc], [RT])
                    yield

        def rw_seq(sub, gi_, gen):
            S = SETS[sub % len(SETS)]
            KAPG, RG, KDIV, BDIV, VB, BON, GT, GC = (S[k_] for k_ in ("KAPG", "RG", "KDIV", "BDIV", "VB", "BON", "GT", "GC"))
            C_ = CTS[sub % len(CTS)]
            VT, KDT, BDT, AKKT, ARKT, ARBT, M_, MT, RT = (C_[k_] for k_ in ("VT", "KDT", "BDT", "AKKT", "ARKT", "ARBT", "M_", "MT", "RT"))
            n0 = sub * TBR
            sq0 = 0
            for j in range(gi_ * gch, (gi_ + 1) * gch if KR > 2 else 0):
                jl = j % gch
                cs = slice(j * c, (j + 1) * c)
                if kind == "s":
                    sq = sq0 + j
                    si = SIN[sq % 2]
                    for sq2 in [sq]:
                        if sq2 < nseq:
                            si2 = SIN[sq2 % 2]
                            DMA(si2.t[:].rearrange("v m (j k) -> v m j k", j=2), st_rS[sq2].rearrange("(m j) v k -> v m j k", j=2), si2, W=[si2])
                    bk = ps()
                    for m in range(4):
                        TR(bk.t[:, m * 64:(m + 1) * 64], si.t[:, m, :], [si], [bk])
                    CP(PST.t[:], bk.t[:, 0:256].rearrange("p (m v) -> p m v", m=4), [bk], [PST])
                if kind == "s" or (sub == 0 and j == 0):
                    CP(PB16.t[:], PST.t[:], [PST], PBL, e="act")
                LP = [(slice(par * 64, par * 64 + 64), slice(par * 64, par * 64 + c)) for par in range(2)]
                for par, (P_, Pc) in enumerate(LP):
                    bkw = ps()
                    for i in range(4):
                        MM(bkw.t[Pc, i * 64:(i + 1) * 64], KAPG.t[P_, i, cs], PB16.t[P_, i, :], True, False, [KAPG, PBL[par]], [bkw])
                        MM(bkw.t[Pc, i * 64:(i + 1) * 64], AKKT.t[Pc, jl, i, 0:c], VT.t[Pc, jl, i, :], False, True, [AKKT, VT], [bkw])
                    CP(WS_.t[Pc].rearrange("p h v -> p (h v)"), bkw.t[Pc, 0:256], [bkw], [LW[par]])
                    pump(gen, PUMPN)
                for par, (P_, Pc) in enumerate(LP):
                    bku = ps()
                    for i in range(4):
                        MM(bku.t[Pc, i * 64:(i + 1) * 64], RT.t[Pc, jl, i, 0:c], WS_.t[Pc, i, :], True, True, [RT, LW[par]], [bku])
                    CP(US.t[Pc].rearrange("p h v -> p (h v)"), bku.t[Pc, 0:256], [bku], [LU[par]], e="act")
                    pump(gen, PUMPN)
                for par, (P_, Pc) in enumerate(LP):
                    bko = ps(); bkp = ps()
                    for i in range(4):
                        oo = bko.t[P_, i * c:(i + 1) * c]
                        MM(oo, PB16.t[P_, i, :], RG.t[P_, i, cs], True, False, [PBL[par], RG], [bko])
                        MM(oo, VT.t[Pc, jl, i, :], ARKT.t[Pc, jl, i, 0:c], False, False, [VT, ARKT], [bko])
                        MM(oo, US.t[Pc, i, :], ARBT.t[Pc, jl, i, 0:c], False, True, [LU[par], ARBT], [bko])
                        pp = bkp.t[P_, i * 64:(i + 1) * 64]
                        MM(pp, IDENT[P_, P_], PST.t[P_, i, :], True, False, CB + [PST], [bkp])
                        MM(pp, KDT.t[Pc, jl, i, :], VT.t[Pc, jl, i, :], False, False, [KDT, VT], [bkp])
                        MM(pp, BDT.t[Pc, jl, i, :], US.t[Pc, i, :], False, True, [BDT, LU[par]], [bkp])
                    CP(OO.t[P_, :, cs], bko.t[P_, 0:4 * c].rearrange("p (m t) -> p m t", m=4), [bko], [OO], e="act")
                    if kind == "p":
                        TT(PB16.t[P_, :, :], bkp.t[P_, 0:256].rearrange("p (m v) -> p m v", m=4), GC.t[P_, :, j:j + 1].to_broadcast([64, 4, 64]), ALU.mult, [bkp, GC], [PBL[par]])
                    TT(PST.t[P_, :, :], bkp.t[P_, 0:256].rearrange("p (m v) -> p m v", m=4), GC.t[P_, :, j:j + 1].to_broadcast([64, 4, 64]), ALU.mult, [bkp, GC], [PST])
                    pump(gen, PUMPN)
                if kind == "s" or (last and sub == nsub - 1 and j == nch - 1):
                    so = SOUT[j % 2]
                    bk = ps()
                    for m in range(4):
                        TR(bk.t[0:64, m * 128:(m + 1) * 128], PST.t[:, m, :], [PST], [bk])
                    CP(so.t[:].rearrange("v m f -> v (m f)"), bk.t[0:64, :], [bk], [so])
                    od = o_srS[sq0 + j] if kind == "s" else o_prS
                    DMA(od.rearrange("(m j) v k -> v m j k", j=2), so.t[:].rearrange("v m (j k) -> v m j k", j=2), so, R=[so])

        def rw_post(sub, gen):
            S = SETS[sub % len(SETS)]
            KAPG, RG, KDIV, BDIV, VB, BON, GT, GC = (S[k_] for k_ in ("KAPG", "RG", "KDIV", "BDIV", "VB", "BON", "GT", "GC"))
            C_ = CTS[sub % len(CTS)]
            VT, KDT, BDT, AKKT, ARKT, ARBT, M_, MT, RT = (C_[k_] for k_ in ("VT", "KDT", "BDT", "AKKT", "ARKT", "ARBT", "M_", "MT", "RT"))
            n0 = sub * TBR
            sq0 = 0
            P.mark(f"{kind}{bi}:rwkv_post{sub}")
            b3v = lambda bk_: bk_.t[:, 0:4 * TBR].rearrange("p (m t) -> p m t", m=4)
            bk = ps()
            for m in range(4):
                MM(bk.t[:, m * TBR:(m + 1) * TBR], BMEAN.t[:], OO.t[:, m, :], True, True, [BMEAN, OO], [bk])
            TT(OO.t[:], OO.t[:], b3v(bk), ALU.subtract, [OO, bk], [OO])
            TT(TGP.t[:], OO.t[:], OO.t[:], ALU.mult, [OO], [TGP])
            pump(gen, PUMPN)
            bk = ps()
            for m in range(4):
                MM(bk.t[:, m * TBR:(m + 1) * TBR], BMEAN.t[:], TGP.t[:, m, :], True, True, [BMEAN, TGP], [bk])
            ACT(TGP.t[:], b3v(bk), AF.Sqrt, [bk], [TGP], bias=64e-5)
            RCP(TGP.t[:], TGP.t[:], [TGP], [TGP])
            TT(OO.t[:], OO.t[:], TGP.t[:], ALU.mult, [OO, TGP], [OO])
            pump(gen, PUMPN)
            TT(OO.t[:], OO.t[:], pbc(pcol("lnx_w", 0, 4), TBR), ALU.mult, [OO, PK], [OO])
            TT(OO.t[:], OO.t[:], pbc(pcol("lnx_b", 0, 4), TBR), ALU.add, [OO, PK], [OO])
            pump(gen, PUMPN)
            TT(OO.t[:], OO.t[:], BON.t[:], ALU.add, [OO, BON], [OO])
            TT(OB.t[:, :, n0:n0 + TBR], OO.t[:], GT.t[:], ALU.mult, [OO, GT], [OB])

        class Multi:
            def __init__(self, gens):
                self.gens = [g for g in gens if g is not None]

            def __next__(self):
                alive = False
                for g in list(self.gens):
                    try:
                        next(g); alive = True
                    except StopIteration:
                        self.gens.remove(g)
                if not alive:
                    raise StopIteration
        ngrp = nch // gch
        if P.dry:
            for sub in range(nsub):
                pump(rw_prep(sub), 10 ** 6)
        elif kind == "s":
            pump(rw_prep(0), 10 ** 6)
            P.mark(f"{kind}{bi}:rwkv_chunks0")
            for gi_ in range(ngrp):
                pump(rw_an(0, gi_), 10 ** 6)
                rw_seq(0, gi_, None)
            rw_post(0, None)
        else:
            pump(rw_prep(0), 10 ** 6)
            pump(Multi([rw_an(0, 0), rw_prep(1) if nsub > 1 else None]), 10 ** 6)
            for sub in range(nsub):
                P.mark(f"{kind}{bi}:rwkv_chunks{sub}")
                gen = Multi([rw_an(sub + 1, 0) if sub + 1 < nsub else None, rw_prep(sub + 2) if sub + 2 < nsub else None])
                rw_seq(sub, 0, gen)
                rw_post(sub, gen)
                pump(gen, 10 ** 6)
        if not P.dry and (kind == "s" or last):
            srcs = [(nm, m) for nm, nt in (("r", 4), ("k", 4), ("v", 4), ("wa", 1), ("g", 1)) for m in range(nt)]
            for g0 in range(0, 14, 4):
                grp = srcs[g0:g0 + 4]
                bk = ps()
                for q, (nm, m) in enumerate(grp):
                    TR(bk.t[0:nseq, q * 128:(q + 1) * 128], sh[nm].t[:, m, :], [sh[nm]], [bk])
                CP(STG.t[0:nseq, 0:len(grp) * 128], bk.t[0:nseq, 0:len(grp) * 128], [bk], [STG])
                od = o_sshift if kind == "s" else o_pshift
                DMA(od[:, g0 * 128:(g0 + len(grp)) * 128], STG.t[0:nseq, 0:len(grp) * 128], STG, R=[STG])

        if LIM <= 2:
            A.off = base
            return
        P.mark(f"{kind}{bi}:wout")
        A.off = m_l0
        P.barrier()

        def oab(k):
            return OA.t[:, k, :] if k < 4 else OB.t[:, k - 4, :]

        ST = {"bk": None, "pend": []}
        SQB = [P.buf(f"sqb{k}") for k in range(8)]

        def stat_mm(k):
            MM(ST["bk"].t[:, 0:TB], ONESB.t[:], xn.t[:, k, 0:TB], k == 0, k == 7, [ONESB, SQB[k]], [ST["bk"]])

        def stat_chunk(k):
            if P.dry or not FUSE_STAT:
                return
            if k == 0:
                ST["bk"] = ps(); PS_RES[0] = psb.index(ST["bk"]); ST["pend"] = []
            ACT(xn.t[:, k, 0:TB], xT.t[:, k, 0:TB], AF.Square, [xT], [SQB[k]] + ([xn] if k == 0 else []))
            ST["pend"].append(k)
            if len(ST["pend"]) > 2:
                stat_mm(ST["pend"].pop(0))

        def rms_finish(gname, dst, mark):
            if P.dry or not FUSE_STAT:
                rmsnorm(xT, gname, dst, TB, mark)
                return
            while ST["pend"]:
                stat_mm(ST["pend"].pop(0))
            A.off = mark
            RS = al([128, TB], F32, "rs")
            bk = ST["bk"]
            ACT(RS.t[:], bk.t[:, 0:TB], AF.Sqrt, [bk], [RS], bias=1e-6, scale=1.0 / 1024.0)
            PS_RES[0] = None
            RCP(RS.t[:], RS.t[:], [RS], [RS])
            for k in range(8):
                STT(dst.t[:, k, 0:TB], xT.t[:, k, 0:TB], pcol(gname, k), RS.t[:], ALU.mult, ALU.mult, [xT, RS, PK], [dst, SQB[k]])
            P.barrier()

        def mk_res(cb):
            def epi(m, bk):
                TT(xT.t[:, cb + m, 0:TB], bk.t[:, 0:TB], xT.t[:, cb + m, 0:TB], ALU.add, [bk, xT], [xT])
                stat_chunk(cb + m)
            return epi
        lin(w_out_ab, 0, 128, 8, 0, 512, oab, [OA, OB], TB, mk_res(0))
        lin(w_out_ab, 0, 128, 8, 512, 512, oab, [OA, OB], TB, mk_res(4))

        def ffn(l):
            P.mark(f"{kind}{bi}:ffn{l}")
            A.off = base
            P.barrier()
            rms_finish(f"ln_ffn{l}", xn, base)
            A.off = base
            HH = al([128, 22, TB], BF16, "HH"); SGt = al([128, TB], F32, "SGt")
            for gi in range(11):
                c0 = gi * 256; ncol = 256
                sg_, sgb = WSTR.get(ffn_gate[l], 0, 128, 8, c0, ncol)
                su_, sub_ = WSTR.get(ffn_up[l], 0, 128, 8, c0, ncol)
                if P.dry:
                    continue
                for m in range(ncol // 128):
                    bg = ps(); bu = ps()
                    for k in range(8):
                        MM(bg.t[:, 0:TB], sg_[:, k, m * 128:(m + 1) * 128], xn.t[:, k, 0:TB], k == 0, k == 7, [sgb, xn], [bg])
                    for k in range(8):
                        MM(bu.t[:, 0:TB], su_[:, k, m * 128:(m + 1) * 128], xn.t[:, k, 0:TB], k == 0, k == 7, [sub_, xn], [bu])
                    ACT(SGt.t[:], bg.t[:, 0:TB], AF.Silu, [bg], [SGt])
                    TT(HH.t[:, gi * 2 + m, :], SGt.t[:], bu.t[:, 0:TB], ALU.mult, [SGt, bu], [HH])
            for m in range(8):
                lin(ffn_down[l], 0, 128, 22, m * 128, 128, lambda k: HH.t[:, k, :], [HH], TB, mk_res(m))
        ffn(0)

        if LIM <= 3:
            A.off = base
            return
        P.mark(f"{kind}{bi}:hgrn")
        A.off = base
        P.barrier()
        rms_finish("ln_mix1", xn, base)
        A.off = base
        OH = OHP; SGG = al([128, 8, TB], BF16, "SGG")
        m_l1 = A.off
        nchb = TB // c
        A.off = m_l1
        P.barrier()
        HSETS = [{"QD": al([128, 4, TB], BF16, "QD"), "KD": al([128, 4, TB], BF16, "KD"),
                  "IT": al([64, nchb, 512], BF16, "IT"), "EBL": al([128, 4, nchb], F32, "EBL")} for _i in range(2)]
        SB162 = [al([128, 4, 128], BF16, "SB16") for _i in range(2)]
        SLO2 = [al([128, 4, 128], BF16, "SLO") for _i in range(2)]
        m_q = A.off
        Q = al([128, 4, TB], F32, "Q"); FF = al([128, 4, TB], F32, "FF"); BC = al([128, 4, TB], F32, "BC")
        EE = al([128, 4, TB], F32, "EE")
        nkb = nchb if kind == "p" else 1
        KDTh2 = [None, None]; ATT2 = [None, None]
        RMh = RM64 if c == 64 else RM4

        def hg_prep(half):
            S = HSETS[half]
            QD, KD, IT, EBL = S["QD"], S["KD"], S["IT"], S["EBL"]
            hh0 = half * 4

            def epi_q(m, bk):
                ACT(Q.t[:, m, :], bk.t[:, 0:TB], AF.Silu, [bk], [Q])
            lin(w_in_c, 0, 128, 8, half * 512, 256, xr, [xn], TB, epi_q); yield
            lin(w_in_c, 0, 128, 8, half * 512 + 256, 256, xr, [xn], TB, lambda m, bk: epi_q(m + 2, bk)); yield

            def epi_f(m, bk):
                ACT(FF.t[:, m, :], bk.t[:, 0:TB], AF.Sigmoid, [bk], [FF])
                TS(FF.t[:, m, :], FF.t[:, m, :], OML(hh0 + m), LB(hh0 + m), ALU.mult, ALU.add, [FF, DER], [FF])
            lin(w_in_c, 0, 128, 8, 1024 + half * 512, 256, xr, [xn], TB, epi_f); yield
            lin(w_in_c, 0, 128, 8, 1024 + half * 512 + 256, 256, xr, [xn], TB, lambda m, bk: epi_f(m + 2, bk)); yield
            for u in range(2):
                slot, sb = WSTR.get(w_in_c, 0, 128, 8, 2048 + half * 512 + u * 256, 256)
                if not P.dry:
                    for j in range(nchb):
                        bk = ps()
                        for k in range(8):
                            MM(bk.t[0:c, 0:256], xn.t[:, k, j * c:(j + 1) * c], slot[:, k, :], k == 0, k == 7, [sb, xn], [bk])
                        CP(IT.t[0:c, j, u * 256:(u + 1) * 256], bk.t[0:c, 0:256], [bk], [IT], e="act")
                        if j % 2 == 1:
                            yield

            def epi_gg(m, bk):
                ACT(SGG.t[:, hh0 + m, 0:TB], bk.t[:, 0:TB], AF.Silu, [bk], [SGG])
            lin(w_in_c, 0, 128, 8, 3072 + half * 512, 256, xr, [xn], TB, epi_gg); yield
            lin(w_in_c, 0, 128, 8, 3072 + half * 512 + 256, 256, xr, [xn], TB, lambda m, bk: epi_gg(m + 2, bk)); yield
            if P.dry:
                return
            fl = lambda t: t.t[:].rearrange("p m t -> p (m t)")
            ACT(EE.t[:], FF.t[:], AF.Ln, [FF], [EE])
            TS(FF.t[:], FF.t[:], -1.0, 1.0, ALU.mult, ALU.add, [FF], [FF])
            yield
            SCAN(fl(BC), RMh.t[:, 0:4 * TB], fl(EE), [RMh, EE], [BC])
            CP(EBL.t[:], BC.t[:].rearrange("p m (j c) -> p m j c", c=c)[:, :, :, c - 1], [BC], [EBL])
            ACT(EBL.t[:], EBL.t[:], AF.Exp, [EBL], [EBL])
            yield
            ACT(EE.t[:], BC.t[:], AF.Exp, [BC], [EE])
            TT(QD.t[:], Q.t[:], EE.t[:], ALU.mult, [Q, EE], [QD])
            yield
            TS(BC.t[:], BC.t[:], -1.0, 80.0, ALU.mult, ALU.min, [BC], [BC])
            ACT(EE.t[:], BC.t[:], AF.Exp, [BC], [EE])
            TT(KD.t[:], FF.t[:], EE.t[:], ALU.mult, [FF, EE], [KD])
            yield

        def hg_pre(half, j):
            S = HSETS[half]
            QD, KD = S["QD"], S["KD"]
            KDTh, ATT = KDTh2[half], ATT2[half]
            cs = slice(j * c, (j + 1) * c)
            bk = ps()
            for m in range(4):
                MM(bk.t[0:c, m * 128:(m + 1) * 128], KD.t[:, m, cs], IDENTB.t[:], True, True, [KD, IDENTB], [bk])
            CP(KDTh.t[0:c, j % nkb].rearrange("p m f -> p (m f)"), bk.t[0:c, :], [bk], [KDTh], e="act")
            bk = ps()
            for m in range(4):
                MM(bk.t[0:c, m * c:(m + 1) * c], KD.t[:, m, cs], QD.t[:, m, cs], True, True, [KD, QD], [bk])
            TT(ATT.t[0:c, j % nkb, :, 0:c], bk.t[0:c, 0:4 * c].rearrange("p (m t) -> p m t", m=4), MI[slice(0, c), slice(0, 4), slice(0, c)], ALU.mult, [bk] + CB, [ATT])

        def hg_chunk(half, j):
            S = HSETS[half]
            QD, KD, IT, EBL = S["QD"], S["KD"], S["IT"], S["EBL"]
            KDTh, ATT, SB16, SLO = KDTh2[half], ATT2[half], SB162[half], SLO2[half]
            hh0 = half * 4
            cs = slice(j * c, (j + 1) * c)
            if kind == "s":
                q_ = 2 * j + half
                St = HSB[q_ % 3]
                if q_ == 0:
                    for qq in (0, 1):
                        DMA(HSB[qq % 3].t[:], st_hS[qq // 2, (qq % 2) * 4:(qq % 2) * 4 + 4].rearrange("h k v -> k h v"), HSB[qq % 3], W=[HSB[qq % 3]])
                qq = q_ + 2
                if qq < 2 * nchb:
                    DMA(HSB[qq % 3].t[:], st_hS[qq // 2, (qq % 2) * 4:(qq % 2) * 4 + 4].rearrange("h k v -> k h v"), HSB[qq % 3], W=[HSB[qq % 3]])
                Sv = lambda m: St.t[:, m, :]
                CP(SB16.t[:], St.t[:], [St], [SB16], e="act")
            else:
                St = HS
                Sv = lambda m: HS.t[:, hh0 + m, :]
                if j == 0:
                    CP(SB16.t[:], HS.t[:, hh0:hh0 + 4, :], [HS], [SB16], e="act")
                    TT(SLO.t[:], HS.t[:, hh0:hh0 + 4, :], SB16.t[:], ALU.subtract, [HS, SB16], [SLO])
            bko = ps(); bks = ps()
            for m in range(4):
                MM(bko.t[:, m * c:(m + 1) * c], IT.t[0:c, j, m * 128:(m + 1) * 128], ATT.t[0:c, j % nkb, m, 0:c], True, False, [IT, ATT], [bko])
                MM(bko.t[:, m * c:(m + 1) * c], SB16.t[:, m, :], QD.t[:, m, cs], False, True, [SB16, QD], [bko])
                if kind == "p":
                    MM(bks.t[:, m * 128:(m + 1) * 128], IDENTB.t[:], SB16.t[:, m, :], True, False, [IDENTB, SB16], [bks])
                    MM(bks.t[:, m * 128:(m + 1) * 128], IDENTB.t[:], SLO.t[:, m, :], False, False, [IDENTB, SLO], [bks])
                MM(bks.t[:, m * 128:(m + 1) * 128], KDTh.t[0:c, j % nkb, m, :], IT.t[0:c, j, m * 128:(m + 1) * 128], kind == "s", True, [KDTh, IT], [bks])
            CP(OH.t[:, hh0:hh0 + 4, cs], bko.t[:, 0:4 * c].rearrange("p (m t) -> p m t", m=4), [bko], [OH], e="act")
            nxt = kind == "p" and j < nchb - 1
            ebc = EBL.t[:, :, j:j + 1].to_broadcast([128, 4, 128])
            b3 = bks.t[:].rearrange("p (m v) -> p m v", m=4)
            if nxt:
                TT(SB16.t[:], b3, ebc, ALU.mult, [bks, EBL], [SB16])
            if kind == "s":
                TT(St.t[:], b3, St.t[:], ALU.add, [bks, St], [St])
                TT(St.t[:], St.t[:], ebc, ALU.mult, [St, EBL], [St])
            else:
                TT(HS.t[:, hh0:hh0 + 4, :], b3, ebc, ALU.mult, [bks, EBL], [St])
                if nxt:
                    TT(SLO.t[:], HS.t[:, hh0:hh0 + 4, :], SB16.t[:], ALU.subtract, [HS, SB16], [SLO])
            if kind == "s":
                DMA(o_shS[j, hh0:hh0 + 4].rearrange("h k v -> k h v"), St.t[:], St, R=[St])

        if kind == "p":
            pump(hg_prep(0), 10 ** 6)
            g1 = hg_prep(1)
            pump(g1, 6 + nchb)
            KDTh2[0] = A.alias(SIN[0], [64, nkb, 4, 128], BF16, "KDTh"); ATT2[0] = al([64, nkb, 4, 64], BF16, "ATT")
            if not P.dry:
                for j in range(nchb):
                    hg_pre(0, j)
                    pump(g1, 1)
            pump(g1, 10 ** 6)
            A.off = m_q
            P.barrier()
            KDTh2[1] = al([64, nkb, 4, 128], BF16, "KDTh"); ATT2[1] = al([64, nkb, 4, 64], BF16, "ATT")
            if not P.dry:
                for j in range(nchb):
                    hg_pre(1, j)
        else:
            for half in range(2):
                pump(hg_prep(half), 10 ** 6)
            A.off = m_q
            P.barrier()
            KDTh2[:] = [al([64, nkb, 4, 128], BF16, "KDTh") for _i in range(2)]; ATT2[:] = [al([64, nkb, 4, 64], BF16, "ATT") for _i in range(2)]
        if not P.dry:
            for j in range(nchb):
                for half in range(2):
                    if kind == "s":
                        hg_pre(half, j)
                    hg_chunk(half, j)
            if last:
                DMA(o_phS.rearrange("h k v -> k h v"), HS.t[:], HS, R=[HS])
        A.off = m_l1
        P.barrier()
        P.mark(f"{kind}{bi}:hgrn_out")
        XN2 = al([128, 8, TB], BF16, "XN2")
        m2 = A.off
        rmsnorm(OH, "gn", XN2, TB, m2)
        if not P.dry:
            TT(XN2.t[:], XN2.t[:], SGG.t[:, :, 0:TB], ALU.mult, [XN2, SGG], [XN2])
        lin(w_out_c, 0, 128, 8, 0, 512, lambda k: XN2.t[:, k, :], [XN2], TB, mk_res(0))
        lin(w_out_c, 0, 128, 8, 512, 512, lambda k: XN2.t[:, k, :], [XN2], TB, mk_res(4))
        ffn(1)

        if LIM <= 4:
            A.off = base
            return
        P.mark(f"{kind}{bi}:final")
        A.off = base
        P.barrier()
        XO = al([128, 8, TB], F32, "XO")
        rms_finish("ln_final", XO, A.off)
        if not P.dry:
            for j in range(ntt):
                for g in range(2):
                    bk = ps()
                    for q in range(4):
                        TR(bk.t[0:rows, q * 128:(q + 1) * 128], XO.t[:, g * 4 + q, j * 128:j * 128 + rows], [XO], [bk])
                    CP(XST.t[0:rows, j, g * 512:(g + 1) * 512], bk.t[0:rows, :], [bk], [XST], e="act")
            if kind == "p":
                DMA(yd.rearrange("(j p) d -> p j d", p=128), XST.t[:], XST, R=[XST])
            else:
                DMA(yd, XST.t[0:64, 0, :], XST, R=[XST])
        A.off = base

    def program():
        if not SAMPLE_LAST and not os.environ.get('KSKIPS'):
            run_block("s", 0)
        for bi in range(NBLK):
            run_block("p", bi)
        if SAMPLE_LAST and not os.environ.get('KSKIPS'):
            run_block("s", 0)

    P.dry = True
    program()
    print("arena hi", A.hi)
    P.dry = False
    WSTR.reset()
    setup()
    program()
    outs = [xtok, xT, STG, SOUT[0], SOUT[1], HS] + HSB
    P.mark("end")
    if os.environ.get("KMARKS"):
        import json
        json.dump(P.marks, open(os.environ["KMARKS"], "w"))
    P.finish([o.b for o in outs])
    P.emit()
    return nc


PK_OFF = {}
PK_N = 0


def _pk_layout():
    global PK_N
    off = 0
    for name, n in (("ln_mix0", 8), ("ln_mix1", 8), ("ln_ffn0", 8), ("ln_ffn1", 8), ("ln_final", 8), ("gn", 8),
                    ("conv_w", 16), ("conv_b", 4), ("gr_b", 4), ("gi_b", 4), ("lam", 4),
                    ("mu_r", 4), ("mu_k", 4), ("mu_v", 4), ("mu_wa", 1), ("mu_g", 1),
                    ("w0", 4), ("a0", 4), ("kkb", 4), ("ka", 4), ("rk", 4), ("lnx_w", 4), ("lnx_b", 4),
                    ("lb0", 8), ("lb1", 8)):
        PK_OFF[name] = off
        off += n
    PK_N = off


_pk_layout()


def _col(v):
    return np.ascontiguousarray(np.asarray(v, np.float32).reshape(-1, 128).T)


def _consts():
    c = np.zeros((128, 256), np.float32)
    cm = np.zeros((128, 320), np.float32)
    c[:, 0:128] = np.eye(128, dtype=np.float32)
    c[0:64, 128:192] = 1.0
    c[64:128, 192:256] = 1.0
    p = np.arange(64)[:, None]; f = np.arange(64)[None, :]
    ms = (f > p).astype(np.float32)
    ml = (f < p).astype(np.float32)
    mi = (f >= p).astype(np.float32)
    idm = (f == p).astype(np.float32)
    for i, mm in enumerate((ms, ml, mi, -mi, idm)):
        cm[0:64, i * 64:(i + 1) * 64] = mm; cm[64:128, i * 64:(i + 1) * 64] = mm
    return c, cm


def _bd(w):
    o = np.zeros((128, 512), np.float32)
    for h in range(8):
        m, j = h // 2, h % 2
        o[j * 64:(j + 1) * 64, m * 128 + j * 64:m * 128 + (j + 1) * 64] = w[h]
    return o


_CACHE = {}


def kernel(**inp):
    f = lambda k: np.asarray(inp[k], np.float32)
    if "nc" not in _CACHE:
        _CACHE["nc"] = build_program()
    nc = _CACHE["nc"]
    pk = np.zeros((128, PK_N), np.float32)

    def put(name, arr):
        a = _col(arr)
        pk[:, PK_OFF[name]:PK_OFF[name] + a.shape[1]] = a
    put("ln_mix0", f("ln_mix")[0]); put("ln_mix1", f("ln_mix")[1])
    put("ln_ffn0", f("ln_ffn")[0]); put("ln_ffn1", f("ln_ffn")[1])
    put("ln_final", f("ln_final")); put("gn", f("gn_c")[0])
    cw = f("conv_w")[0]
    pk[:, PK_OFF["conv_w"]:PK_OFF["conv_w"] + 16] = np.concatenate([_col(cw[j]) for j in range(4)], axis=1)
    put("conv_b", f("conv_b")[0]); put("gr_b", f("gr_b")[0]); put("gi_b", f("gi_b")[0]); put("lam", f("lru_lambda")[0])
    mu = f("mu_b")[0]
    put("mu_r", mu[0:512]); put("mu_k", mu[512:1024]); put("mu_v", mu[1024:1536])
    put("mu_wa", mu[1536:1664]); put("mu_g", mu[1664:1792])
    put("w0", f("w0_b")[0]); put("a0", f("a0_b")[0]); put("kkb", f("kk_b")[0]); put("ka", f("ka_b")[0])
    put("rk", f("rk_b")[0].reshape(-1)); put("lnx_w", f("lnx_w")[0]); put("lnx_b", f("lnx_b")[0])
    put("lb0", f("lb_c")[0]); put("lb1", f("lb_c")[1])
    shared = {
        "w_in_ab": f("w_in_ab")[0], "w_out_ab": f("w_out_ab")[0], "w_in_c": f("w_in_c")[0], "w_out_c": f("w_out_c")[0],
        "w2": f("w2_b")[0], "a2": f("a2_b")[0], "g2": f("g2_b")[0],
        "grw_bd": _bd(f("gr_w")[0]), "giw_bd": _bd(f("gi_w")[0]), "pack128": pk, "consts": _consts()[0], "cmasks": _consts()[1],
    }
    for l in range(2):
        shared[f"ffn_gate{l}"] = f("ffn_gate")[l]; shared[f"ffn_up{l}"] = f("ffn_up")[l]; shared[f"ffn_down{l}"] = f("ffn_down")[l]
    shared = {k: np.ascontiguousarray(v) for k, v in shared.items()}
    in_maps = []
    for i in range(NCORES):
        s = slice(16 * i, 16 * i + 16)
        m = dict(shared)
        m["xp"] = np.ascontiguousarray(f("x_prompt")[i]); m["xs"] = np.ascontiguousarray(f("x_sample")[s].reshape(64, D))
        m["st_conv"] = np.ascontiguousarray(f("state_rglru_conv")[0, s]); m["st_h"] = np.ascontiguousarray(f("state_rglru_h")[0, s])
        m["st_shift"] = np.ascontiguousarray(f("state_rwkv_shift")[0, s]); m["st_rS"] = np.ascontiguousarray(f("state_rwkv_S")[0, s])
        m["st_hS"] = np.ascontiguousarray(f("state_hgrn_S")[0, s])
        in_maps.append(m)
    res = run_bass_kernel_spmd(nc, in_maps, core_ids=list(range(NCORES)))
    R = res.results
    cat = lambda k, shp: np.concatenate([np.asarray(r[k], np.float32).reshape(shp) for r in R], axis=0)[None]
    y_prompt = np.stack([np.asarray(r["yp"], np.float32) for r in R], axis=0)
    y_sample = np.concatenate([np.asarray(r["ys"], np.float32).reshape(16, 4, D) for r in R], axis=0)
    return (y_prompt, y_sample,
            cat("p_conv", (1, 3, 512)), cat("p_h", (1, 512)), cat("p_shift", (1, 1792)),
            cat("p_rS", (1, 8, 64, 64)), cat("p_hS", (1, 8, 128, 128)),
            cat("s_conv", (16, 3, 512)), cat("s_h", (16, 512)), cat("s_shift", (16, 1792)),
            cat("s_rS", (16, 8, 64, 64)), cat("s_hS", (16, 8, 128, 128)))
```

```python
import numpy as np
import concourse.bass as bass
import concourse.mybir as mybir
from concourse.bass_utils import run_bass_kernel_spmd
from concourse.alu_op_type import AluOpType as ALU

F32 = mybir.dt.float32
BF16 = mybir.dt.bfloat16
AF = mybir.ActivationFunctionType
NCORES = 8
import os
LIM = int(os.environ.get('KLIM', '99'))
NBLK = int(os.environ.get('KNBLK', '4'))
KSUB = int(os.environ.get('KSUB', '99'))
FUSE_STAT = int(os.environ.get('KFUSE', '1'))
WCACHE = int(os.environ.get('KWCACHE', '1'))
SAMPLE_LAST = int(os.environ.get('KSLAST', '1'))
PUMPN = int(os.environ.get('KPUMP', '1'))
SAME_SYNC = int(os.environ.get('KSAME', '1'))
KR = int(os.environ.get('KR', '99'))
KA = int(os.environ.get('KA', '99'))
D = 1024
TBP = 512
EW = 0.6065306597126334


class Buf:
    __slots__ = ("name", "w", "r", "dsem", "dcnt")

    def __init__(self, name):
        self.name = name; self.w = None; self.r = {}; self.dsem = None; self.dcnt = 0


class Prog:
    def __init__(self, nc):
        self.nc = nc
        self.engs = {"pe": nc.tensor, "act": nc.scalar, "dve": nc.vector, "pool": nc.gpsimd, "sp": nc.sync}
        self.q = {e: [] for e in self.engs}
        self.esem = {e: nc.alloc_semaphore("es_" + e) for e in self.engs}
        self.ecnt = {e: 0 for e in self.engs}
        self.seen = {e: {} for e in self.engs}
        self.pend = {e: [] for e in self.engs}
        self.dry = False
        self.nbuf = 0
        self.marks = []
        self.pe_ops = []
        self.pe_inc_idx = []

    def _pe_resolve(self, idx):
        import bisect
        k = bisect.bisect_left(self.pe_inc_idx, idx)
        if k < len(self.pe_inc_idx):
            return k + 1
        last = len(self.pe_ops) - 1
        self.pe_ops[last]["inc"] = True
        self.pe_inc_idx.append(last)
        return len(self.pe_inc_idx)

    def buf(self, name=None):
        self.nbuf += 1
        return Buf(name or f"b{self.nbuf}")

    def mark(self, label):
        if not self.dry:
            self.marks.append((label, dict(self.ecnt)))

    def barrier(self):
        if self.dry:
            return
        toks = [(self.esem[e], self.ecnt[e]) for e in ("act", "dve", "pool") if self.ecnt[e] > 0]
        if self.pe_ops:
            toks.append(("PE", len(self.pe_ops) - 1))
        for e in ("pe", "act", "dve", "pool"):
            self.pend[e] = list(toks)

    def _deps(self, e, reads, writes):
        deps = {}

        def add(tok):
            if tok is None:
                return
            s, v = tok
            if s == "PE":
                if e == "pe":
                    return
                s, v = self.esem["pe"], self._pe_resolve(v)
            k = id(s)
            if k not in deps or deps[k][1] < v:
                deps[k] = (s, v)
        for b in reads:
            add(b.w)
        for b in writes:
            add(b.w)
            for t in b.r.values():
                add(t)
        for t in self.pend[e]:
            add(t)
        self.pend[e] = []
        out = []
        seen = self.seen[e]
        for k, (s, v) in deps.items():
            if s is self.esem[e] and (e == "pe" or not SAME_SYNC):
                continue
            if seen.get(k, 0) >= v:
                continue
            seen[k] = v
            out.append((s, v))
        return out

    def _mark(self, tok, reads, writes):
        k = "PE" if tok[0] == "PE" else id(tok[0])
        for b in reads:
            b.r[k] = tok
        for b in writes:
            b.w = tok; b.r = {}

    def op(self, e, fn, reads=(), writes=()):
        if self.dry:
            return
        deps = self._deps(e, reads, writes)
        self.ecnt[e] += 1
        sem = self.esem[e]
        if e == "pe":
            rec = {"inc": False}
            self.pe_ops.append(rec)
            tok = ("PE", len(self.pe_ops) - 1)
            self._mark(tok, reads, writes)

            def run(eng):
                for s, v in deps:
                    eng.wait_ge(s, v)
                ins = fn(eng)
                if rec["inc"]:
                    ins.then_inc(sem, 1)
            self.q[e].append(run)
            return
        tok = (self.esem[e], self.ecnt[e])
        self._mark(tok, reads, writes)

        def run(eng):
            for s, v in deps:
                eng.wait_ge(s, v)
            fn(eng).then_inc(sem, 1)
        self.q[e].append(run)

    def dma(self, e, out, in_, tb, reads=(), writes=(), **kw):
        if self.dry:
            return
        deps = self._deps(e, reads, writes)
        if tb.dsem is None:
            tb.dsem = self.nc.alloc_semaphore("ds_" + tb.name)
        tb.dcnt += 1
        tok = (tb.dsem, 16 * tb.dcnt)
        self._mark(tok, reads, writes)
        dsem = tb.dsem

        def run(eng):
            for s, v in deps:
                eng.wait_ge(s, v)
            eng.dma_start(out=out, in_=in_, **kw).then_inc(dsem, 16)
        self.q[e].append(run)

    def finish(self, out_bufs):
        deps = self._deps("sp", (), out_bufs)

        def run(eng):
            for s, v in deps:
                eng.wait_ge(s, v)
        self.q["sp"].append(run)

    def emit(self):
        nc = self.nc
        with nc.Block() as block:
            @block.tensor
            def _(eng):
                for f in self.q["pe"]:
                    f(eng)

            @block.scalar
            def _(eng):
                for f in self.q["act"]:
                    f(eng)

            @block.vector
            def _(eng):
                for f in self.q["dve"]:
                    f(eng)

            @block.gpsimd
            def _(eng):
                for f in self.q["pool"]:
                    f(eng)

            @block.sync
            def _(eng):
                for f in self.q["sp"]:
                    f(eng)


class T_:
    __slots__ = ("t", "b")

    def __init__(self, t, b):
        self.t = t; self.b = b


def build_program():
    nc = bass.Bass("TRN2", target_bir_lowering=False)
    P = Prog(nc)
    dram_in = {}
    dram_out = {}

    def din(name, shape):
        dram_in[name] = nc.dram_tensor(name, list(shape), F32, kind="ExternalInput").ap()
        return dram_in[name]

    def dout(name, shape):
        dram_out[name] = nc.dram_tensor(name, list(shape), F32, kind="ExternalOutput").ap()
        return dram_out[name]

    xp = din("xp", [2048, D]); xs = din("xs", [64, D])
    st_conv = din("st_conv", [16, 3, 512]); st_h = din("st_h", [16, 512]); st_shift = din("st_shift", [16, 1792])
    st_rS = din("st_rS", [16, 8, 64, 64]); st_hS = din("st_hS", [16, 8, 128, 128])
    w_in_ab = din("w_in_ab", [1024, 2816]); w_out_ab = din("w_out_ab", [1024, 1024])
    w_in_c = din("w_in_c", [1024, 4096]); w_out_c = din("w_out_c", [1024, 1024])
    ffn_gate = [din(f"ffn_gate{l}", [1024, 2816]) for l in range(2)]
    ffn_up = [din(f"ffn_up{l}", [1024, 2816]) for l in range(2)]
    ffn_down = [din(f"ffn_down{l}", [2816, 1024]) for l in range(2)]
    w2d = din("w2", [64, 512]); a2d = din("a2", [64, 512]); g2d = din("g2", [128, 512])
    grwd = din("grw_bd", [128, 512]); giwd = din("giw_bd", [128, 512])
    pk_d = din("pack128", [128, PK_N]); cst_d = din("consts", [128, 256]); cstm_d = din("cmasks", [128, 320])
    yp = dout("yp", [2048, D]); ys = dout("ys", [64, D])
    o_pconv = dout("p_conv", [3, 512]); o_ph = dout("p_h", [1, 512]); o_pshift = dout("p_shift", [1, 1792])
    o_prS = dout("p_rS", [8, 64, 64]); o_phS = dout("p_hS", [8, 128, 128])
    o_sconv = dout("s_conv", [16, 3, 512]); o_sh = dout("s_h", [16, 512]); o_sshift = dout("s_shift", [16, 1792])
    o_srS = dout("s_rS", [16, 8, 64, 64]); o_shS = dout("s_hS", [16, 8, 128, 128])

    class Arena:
        def __init__(self):
            self.off = 16512; self.n = 0; self.hi = 0; self.offs = {}

        def alloc(self, shape, dt, name=None):
            esz = 4 if dt == F32 else 2
            nb = esz
            for s in shape[1:]:
                nb *= s
            off = (self.off + 31) // 32 * 32
            self.off = off + nb
            self.hi = max(self.hi, self.off)
            assert self.off <= 229000, ("SBUF overflow", self.off, name)
            self.n += 1
            nm = f"{name or 't'}_{self.n}"
            t_ = T_(nc.alloc_sbuf_tensor_at(nm, list(shape), dt, offset=off), P.buf(nm))
            self.offs[id(t_)] = off
            return t_

        def alias(self, t_, shape, dt, name="alias"):
            self.n += 1
            return T_(nc.alloc_sbuf_tensor_at(f"{name}_{self.n}", list(shape), dt, offset=self.offs[id(t_)]), t_.b)
    A = Arena()
    al = A.alloc

    PK = al([128, PK_N], F32, "pk"); CST = al([128, 256], F32, "cst"); CSTM = al([128, 320], F32, "cstm")
    xT = al([128, 8, TBP], F32, "xT"); xn = al([128, 8, TBP], BF16, "xn")
    XST = A.alias(xT, [128, 4, D], F32, "xst")
    xtok = al([128, 4, D], F32, "xtok")
    A.n += 1
    OHP = T_(nc.alloc_sbuf_tensor_at(f"ohp_{A.n}", [128, 8, TBP], F32, offset=A.off - 16384), xtok.b)
    NSLOT = 10
    SLOT = 2048
    slots = [al([128, SLOT], BF16, f"slot{i}") for i in range(NSLOT)]
    W2 = al([64, 512], BF16, "W2"); A2 = al([128, 512], BF16, "A2"); G2 = al([128, 512], BF16, "G2")
    GRW = al([128, 512], BF16, "GRW"); GIW = al([128, 512], BF16, "GIW")
    ONESB = al([128, 128], BF16, "onesb"); IDENTB = al([128, 128], BF16, "identb")
    RKM = al([128, 4, 128], F32, "rkm"); BMEAN = al([128, 128], F32, "bmean")
    DER = al([128, 40], F32, "der")
    RM64 = al([128, 2048], BF16, "rm64"); RM4 = al([128, 512], BF16, "rm4")
    PAH = al([128, 4, 3], F32, "pah"); pA_s = al([128, 4, 16, 7], F32, "pA_s")
    H0 = {"p": al([128, 4, 1], F32, "h0p"), "s": al([128, 4, 16], F32, "h0s")}
    SH = {k: {"r": al([128, 4, n], F32), "k": al([128, 4, n], F32), "v": al([128, 4, n], F32),
              "wa": al([128, 1, n], F32), "g": al([128, 1, n], F32)} for k, n in (("p", 1), ("s", 16))}
    PST = al([128, 4, 64], F32, "pst")
    HS = al([128, 8, 128], F32, "hs")
    SIN = [al([64, 4, 128], F32, f"sin{i}") for i in range(2)]
    SOUT = [al([64, 4, 128], F32, f"sout{i}") for i in range(2)]
    HSB = [al([128, 4, 128], F32, f"hsb{i}") for i in range(3)]
    STG = al([64, 512], F32, "stg")
    psb = [T_(nc.alloc_psum_tensor(f"ps{i}", [128, 512], F32), P.buf(f"ps{i}")) for i in range(8)]
    psi = [0]

    PS_RES = [None]

    def ps():
        psi[0] = (psi[0] + 1) % 8
        if psi[0] == PS_RES[0]:
            psi[0] = (psi[0] + 1) % 8
        return psb[psi[0]]

    IDENT = CST.t[:, 0:128]
    BONES = CST.t[:, 128:256]
    class BMask:
        def __init__(self, i):
            self.i = i

        def __getitem__(self, key):
            p_, h_, t_ = key
            base = CSTM.t[p_, self.i * 64 + t_.start:self.i * 64 + t_.stop]
            return base.rearrange("p (o t) -> p o t", o=1).to_broadcast([p_.stop - p_.start, h_.stop - h_.start, t_.stop - t_.start])
    MS, ML, MI, MIN_, IDM = (BMask(i) for i in range(5))
    CB = [CST.b, CSTM.b]

    def pcol(name, j=0, n=1):
        o = PK_OFF[name] + j
        return PK.t[:, o:o + n]

    def pbc(ap2, n):
        return ap2.rearrange("p (m o) -> p m o", o=1).to_broadcast([128, 4, n])

    def bl(xs_):
        return [x.b if isinstance(x, T_) else x for x in xs_]

    def TT(out, a, b, op, R, W, e="dve"):
        P.op(e, lambda en: en.tensor_tensor(out=out, in0=a, in1=b, op=op), bl(R), bl(W))

    def TS(out, a, s1, s2, op0, op1, R, W, e="dve"):
        if op1 is None:
            P.op(e, lambda en: en.tensor_scalar(out=out, in0=a, scalar1=s1, scalar2=None, op0=op0), bl(R), bl(W))
        else:
            P.op(e, lambda en: en.tensor_scalar(out=out, in0=a, scalar1=s1, scalar2=s2, op0=op0, op1=op1), bl(R), bl(W))

    def STT(out, a, s, b, op0, op1, R, W):
        P.op("dve", lambda en: en.scalar_tensor_tensor(out=out, in0=a, scalar=s, in1=b, op0=op0, op1=op1), bl(R), bl(W))

    def ACT(out, a, func, R, W, bias=None, scale=None):
        kw = {}
        if bias is not None:
            kw["bias"] = bias
        if scale is not None:
            kw["scale"] = scale
        P.op("act", lambda en: en.activation(out=out, in_=a, func=func, **kw), bl(R), bl(W))

    def CP(out, a, R, W, e="dve"):
        if e == "act":
            ACT(out, a, AF.Copy, R, W)
        else:
            P.op(e, lambda en: en.tensor_copy(out=out, in_=a), bl(R), bl(W))

    def MSET(ap, val, W, e="dve"):
        P.op(e, lambda en: en.memset(ap, val), [], bl(W))

    def RCP(out, a, R, W):
        P.op("dve", lambda en: en.reciprocal(out=out, in_=a), bl(R), bl(W))

    def MM(out, lhsT, rhs, st, sp_, R, W):
        P.op("pe", lambda en: en.matmul(out, lhsT, rhs, start=st, stop=sp_), bl(R), bl(W))

    def TRX(out, in_, R, W):
        n = in_.shape[0]
        assert in_.shape[1] == 128
        P.op("pe", lambda en: en.transpose(out, in_, IDENT[0:n, 0:n]), bl(R) + CB, bl(W))

    def TR(out, in_, R, W):
        n = in_.shape[0]
        P.op("pe", lambda en: en.matmul(out, in_, IDENT[0:n, 0:n], start=True, stop=True), bl(R) + CB, bl(W))

    def SCAN(out, d0, d1, R, W):
        P.op("dve", lambda en: en.tensor_tensor_scan(out=out, data0=d0, data1=d1, initial=0.0, op0=ALU.mult, op1=ALU.add), bl(R), bl(W))

    def DMA(out, in_, tb, R=(), W=(), e="sp", **kw):
        P.dma(e, out, in_, tb.b if isinstance(tb, T_) else tb, bl(R), bl(W), **kw)

    class WStream:
        def __init__(self):
            self.units = []; self.i = 0; self.issued = 0

        def reset(self):
            self.i = 0; self.issued = 0
            self.off = {}; tot = 0
            for u in self.units:
                key = (id(u[0]),) + tuple(u[1:])
                if key not in self.off:
                    self.off[key] = tot; tot += u[3] * u[5]
            self.scr = nc.dram_tensor("wscr", [128, tot], BF16).ap()
            self.sbufs = {k: P.buf("scr") for k in self.off}
            self.done = set()
            self.wb = [P.buf(f"wb{i}") for i in range(NSLOT)]

        def _issue(self, j):
            wd, r0, pk, kc, c0, ncol = self.units[j]
            key = (id(wd), r0, pk, kc, c0, ncol)
            s = slots[j % NSLOT]
            n_ = kc * ncol; off = self.off[key]
            if WCACHE and key in self.done:
                DMA(s.t[0:pk, 0:n_], self.scr[0:pk, off:off + n_], s, R=[self.sbufs[key]], W=[s], e="pool")
                return
            dst = s.t[0:pk, 0:n_].rearrange("p (k n) -> p k n", k=kc)
            src = wd[r0:r0 + kc * pk, c0:c0 + ncol].rearrange("(k p) n -> p k n", p=pk)
            DMA(dst, src, s, W=[s], e="pool")
            if WCACHE:
                DMA(self.scr[0:pk, off:off + n_], s.t[0:pk, 0:n_], self.wb[j % NSLOT], R=[s], W=[self.sbufs[key]], e="sp")
                self.done.add(key)

        def get(self, wd, r0, pk, kc, c0, ncol):
            assert kc * ncol <= SLOT
            if P.dry:
                self.units.append((wd, r0, pk, kc, c0, ncol))
                return None, None
            j = self.i
            assert self.units[j][1:] == (r0, pk, kc, c0, ncol)
            self.i += 1
            lim = min(len(self.units), j + NSLOT - 2)
            while self.issued < lim:
                self._issue(self.issued); self.issued += 1
            s = slots[j % NSLOT]
            return s.t[0:pk, 0:kc * ncol].rearrange("p (k n) -> p k n", k=kc), s
    WSTR = WStream()

    def setup():
        DMA(PK.t[:], pk_d, PK, W=[PK]); DMA(CST.t[:], cst_d, CST, W=[CST]); DMA(CSTM.t[:], cstm_d, CSTM, W=[CSTM])
        DMA(W2.t[:], w2d, W2, W=[W2], e="pool"); DMA(A2.t[64:128, :], a2d, A2, W=[A2], e="pool")
        DMA(G2.t[:], g2d, G2, W=[G2], e="pool"); DMA(GRW.t[:], grwd, GRW, W=[GRW], e="pool")
        DMA(GIW.t[:], giwd, GIW, W=[GIW], e="pool")
        MSET(ONESB.t[:], 1.0, [ONESB])
        CP(IDENTB.t[:], IDENT, CB, [IDENTB])
        MSET(RM64.t[:], 1.0, [RM64]); MSET(RM4.t[:], 1.0, [RM4])
        MSET(RM64.t[:].rearrange("p (j c) -> p j c", c=64)[:, :, 0:1], 0.0, [RM64])
        MSET(RM4.t[:].rearrange("p (j c) -> p j c", c=4)[:, :, 0:1], 0.0, [RM4])
        TS(BMEAN.t[:], BONES, 1.0 / 64.0, None, ALU.mult, None, CB, [BMEAN])
        for m in range(4):
            TS(RKM.t[:, m, :], BONES, pcol("rk", m), None, ALU.mult, None, CB + [PK], [RKM])
        ACT(DER.t[:, 0:4], pcol("lam", 0, 4), AF.Exp, [PK], [DER], scale=-1.0)
        ACT(DER.t[:, 0:4], DER.t[:, 0:4], AF.Ln, [DER], [DER], bias=1.0)
        TS(DER.t[:, 4:8], DER.t[:, 0:4], -16.0, None, ALU.mult, None, [DER], [DER])
        TS(DER.t[:, 0:4], DER.t[:, 0:4], -8.0, None, ALU.mult, None, [DER], [DER])
        TT(DER.t[:, 8:16], pcol("lb1", 0, 8), pcol("lb0", 0, 8), ALU.subtract, [PK], [DER])
        ACT(DER.t[:, 8:16], DER.t[:, 8:16], AF.Sigmoid, [DER], [DER])
        TS(DER.t[:, 16:24], DER.t[:, 8:16], -1.0, 1.0, ALU.mult, ALU.add, [DER], [DER])
        TS(DER.t[:, 24:28], pcol("ka", 0, 4), -1.0, 1.0, ALU.mult, ALU.add, [PK], [DER])
        for t in (PAH, H0["p"], PST, HS) + tuple(SH["p"].values()):
            MSET(t.t[:], 0.0, [t])

    NSP = lambda m: DER.t[:, m:m + 1]
    NSP2 = lambda m: DER.t[:, 4 + m:5 + m]
    LB = lambda h: DER.t[:, 8 + h:9 + h]
    OML = lambda h: DER.t[:, 16 + h:17 + h]
    OMKA = lambda m: DER.t[:, 24 + m:25 + m]

    def rmsnorm(src, gname, dst, TB, mark):
        A.off = mark
        SQ = al([128, 8, TB], BF16, "sq"); RS = al([128, TB], F32, "rs")
        ACT(SQ.t[:], src.t[:, :, 0:TB], AF.Square, [src], [SQ])
        bk = ps()
        for k in range(8):
            MM(bk.t[:, 0:TB], ONESB.t[:], SQ.t[:, k, :], k == 0, k == 7, [ONESB, SQ], [bk])
        ACT(RS.t[:], bk.t[:, 0:TB], AF.Sqrt, [bk], [RS], bias=1e-6, scale=1.0 / 1024.0)
        RCP(RS.t[:], RS.t[:], [RS], [RS])
        for k in range(8):
            STT(dst.t[:, k, 0:TB], src.t[:, k, 0:TB], pcol(gname, k), RS.t[:], ALU.mult, ALU.mult, [src, RS, PK], [dst])
        P.barrier()

    def lin(wd, r0, pk, kc, c0, ncol, rhs, rbufs, N, epi, mch=128):
        if kc > 11:
            assert ncol == 128 and kc % 2 == 0
            kh = kc // 2
            s0, b0 = WSTR.get(wd, r0, pk, kh, c0, ncol)
            s1, b1 = WSTR.get(wd, r0 + kh * pk, pk, kh, c0, ncol)
            if P.dry:
                return
            bk = ps()
            for k in range(kc):
                sl_, sb_ = (s0, b0) if k < kh else (s1, b1)
                MM(bk.t[0:mch, 0:N], sl_[:, k % kh, 0:mch], rhs(k), k == 0, k == kc - 1, [sb_] + rbufs, [bk])
            epi(0, bk)
            return
        uw = 256
        for u in range((ncol + uw - 1) // uw):
            w_ = min(uw, ncol - u * uw)
            slot, sb = WSTR.get(wd, r0, pk, kc, c0 + u * uw, w_)
            if P.dry:
                continue
            for m in range(w_ // mch):
                bk = ps()
                for k in range(kc):
                    MM(bk.t[0:mch, 0:N], slot[:, k, m * mch:(m + 1) * mch], rhs(k), k == 0, k == kc - 1, [sb] + rbufs, [bk])
                epi(u * (uw // mch) + m, bk)

    def run_block(kind, bi):
        if kind == "p":
            nseq, L, TB, c, TBR = 1, 512, 512, 64, 128
            xd = xp[bi * 512:(bi + 1) * 512, :]; yd = yp[bi * 512:(bi + 1) * 512, :]
            pA = None
        else:
            nseq, L, TB, c, TBR = 16, 4, 64, 4, 64
            xd = xs; yd = ys
            pA = pA_s
        nlev = {64: 5, 4: 1}[c]
        RM = RM64 if c == 64 else RM4
        h0 = H0[kind]; sh = SH[kind]
        last = (kind == "p" and bi == 3)
        base = A.off
        P.barrier()

        if kind == "p" and bi == 0:
            MSET(PST.t[:], 0.0, [PST])
        P.mark(f"{kind}{bi}:load")
        ntt = max(1, TB // 128); rows = min(128, TB)
        if kind == "p":
            DMA(xtok.t[:], xd.rearrange("(j p) d -> p j d", p=128), xtok, W=[xtok])
        else:
            DMA(xtok.t[0:64, 0, :], xd, xtok, W=[xtok])
        for j in range(ntt):
            for g in range(2):
                bk = ps()
                for q in range(4):
                    TRX(bk.t[:, q * 128:q * 128 + rows], xtok.t[0:rows, j, (g * 4 + q) * 128:(g * 4 + q + 1) * 128], [xtok], [bk])
                CP(xT.t[:, g * 4:g * 4 + 4, j * 128:j * 128 + rows],
                   bk.t[:].rearrange("p (q t) -> p q t", q=4)[:, :, 0:rows], [bk], [xT], e="act")

        if kind == "s":
            DMA(STG.t[0:48, 0:512], st_conv.rearrange("s j f -> (s j) f"), STG, W=[STG])
            bk = ps()
            for m in range(4):
                TR(bk.t[:, m * 48:(m + 1) * 48], STG.t[0:48, m * 128:(m + 1) * 128], [STG], [bk])
            for m in range(4):
                CP(pA.t[:, m, :, 0:3], bk.t[:, m * 48:(m + 1) * 48].rearrange("p (s j) -> p s j", j=3), [bk], [pA])
            DMA(STG.t[0:16, 0:512], st_h, STG, W=[STG])
            bk = ps()
            for m in range(4):
                TR(bk.t[:, m * 16:(m + 1) * 16], STG.t[0:16, m * 128:(m + 1) * 128], [STG], [bk])
            CP(h0.t[:], bk.t[:, 0:64].rearrange("p (m s) -> p m s", m=4), [bk], [h0])
            bk = ps()
            for g0 in range(0, 14, 4):
                w_ = min(4, 14 - g0)
                DMA(STG.t[0:16, 0:w_ * 128], st_shift[:, g0 * 128:(g0 + w_) * 128], STG, W=[STG])
                for q in range(w_):
                    TR(bk.t[:, (g0 + q) * 16:(g0 + q + 1) * 16], STG.t[0:16, q * 128:(q + 1) * 128], [STG], [bk])
            for ii, nm in enumerate(("r", "k", "v")):
                CP(sh[nm].t[:], bk.t[:, ii * 64:(ii + 1) * 64].rearrange("p (m s) -> p m s", m=4), [bk], [sh[nm]])
            CP(sh["wa"].t[:, 0, :], bk.t[:, 192:208], [bk], [sh["wa"]])
            CP(sh["g"].t[:, 0, :], bk.t[:, 208:224], [bk], [sh["g"]])

        if LIM <= 0:
            A.off = base
            return
        P.mark(f"{kind}{bi}:L0norm+rglru")
        rmsnorm(xT, "ln_mix0", xn, TB, base)
        A.off = base
        OA = al([128, 4, TB], BF16, "OA"); OB = al([128, 4, TB], BF16, "OB")
        m_l0 = A.off
        if kind == "p":
            pA = al([128, 4, 1, 515], F32, "pA")
            CP(pA.t[:, :, 0, 0:3], PAH.t[:], [PAH], [pA])
        GA = al([128, 4, TB], F32, "GA")
        U4 = [al([128, TB], F32, "U") for _i in range(4)]; UB4 = [al([128, TB], BF16, "UB") for _i in range(4)]
        R4 = [al([128, TB], F32, "R_") for _i in range(4)]; IG4 = [al([128, TB], F32, "IG") for _i in range(4)]
        AA4 = [al([128, TB], F32, "AA_") for _i in range(4)]; S4 = [al([128, TB], F32, "S_") for _i in range(4)]
        HH4 = [al([128, TB], F32, "HH_") for _i in range(4)]; TM4 = [al([128, 16], F32, "tmps") for _i in range(4)]
        U = U4[0]; TMPS = TM4[0]
        xr = lambda k: xn.t[:, k, 0:TB]

        def epi_x(m, bk):
            CP(pA.t[:, m, :, 3:3 + L], bk.t[:, 0:TB].rearrange("p (s l) -> p s l", l=L), [bk], [pA], e="act")
        if KSUB <= 0:
            A.off = base
            return
        lin(w_in_ab, 0, 128, 8, 0, 512, xr, [xn], TB, epi_x)
        if KSUB <= 1:
            A.off = base
            return

        def epi_g(m, bk):
            ACT(U.t[:], bk.t[:, 0:TB], AF.Square, [bk], [U])
            TS(U.t[:], U.t[:], 0.044715, 1.0, ALU.mult, ALU.add, [U], [U])
            TT(U.t[:], U.t[:], bk.t[:, 0:TB], ALU.mult, [U, bk], [U])
            ACT(U.t[:], U.t[:], AF.Sigmoid, [U], [U], scale=1.5957691216057308)
            TT(GA.t[:, m, :], U.t[:], bk.t[:, 0:TB], ALU.mult, [U, bk], [GA])
        lin(w_in_ab, 0, 128, 8, 512, 512, xr, [xn], TB, epi_g)
        if KSUB <= 2:
            A.off = base
            return
        if not P.dry:
            v3 = lambda t: t.t[:].rearrange("p (s l) -> p s l", l=L)
            M4 = range(4)
            for m in M4:
                TS(v3(U4[m]), pA.t[:, m, :, 0:L], pcol("conv_w", m), pcol("conv_b", m), ALU.mult, ALU.add, [pA, PK], [U4[m]])
            for j in range(1, 4):
                for m in M4:
                    STT(v3(U4[m]), pA.t[:, m, :, j:j + L], pcol("conv_w", 4 * j + m), v3(U4[m]), ALU.mult, ALU.add, [pA, PK, U4[m]], [U4[m]])
            for m in M4:
                CP(UB4[m].t[:], U4[m].t[:], [U4[m]], [UB4[m]], e="act")
            for m in M4:
                bk = ps()
                MM(bk.t[:, 0:TB], GRW.t[:, m * 128:(m + 1) * 128], UB4[m].t[:], True, True, [GRW, UB4[m]], [bk])
                ACT(R4[m].t[:], bk.t[:, 0:TB], AF.Sigmoid, [bk, PK], [R4[m]], bias=pcol("gr_b", m))
            for m in M4:
                bk = ps()
                MM(bk.t[:, 0:TB], GIW.t[:, m * 128:(m + 1) * 128], UB4[m].t[:], True, True, [GIW, UB4[m]], [bk])
                ACT(IG4[m].t[:], bk.t[:, 0:TB], AF.Sigmoid, [bk, PK], [IG4[m]], bias=pcol("gi_b", m))
            for m in M4:
                TT(IG4[m].t[:], IG4[m].t[:], U4[m].t[:], ALU.mult, [IG4[m], U4[m]], [IG4[m]])
            for m in M4:
                ACT(AA4[m].t[:], R4[m].t[:], AF.Exp, [R4[m], DER], [AA4[m]], scale=NSP(m))
            for m in M4:
                ACT(S4[m].t[:], R4[m].t[:], AF.Exp, [R4[m], DER], [S4[m]], scale=NSP2(m))
            for m in M4:
                ACT(S4[m].t[:], S4[m].t[:], AF.Sqrt, [S4[m]], [S4[m]], bias=1.0, scale=-1.0)
            for m in M4:
                TT(TM4[m].t[:, 0:nseq], v3(AA4[m])[:, :, 0], h0.t[:, m, :], ALU.mult, [AA4[m], h0], [TM4[m]])
            for m in M4:
                MSET(v3(AA4[m])[:, :, 0:1], 0.0, [AA4[m]])
            for m in M4:
                TT(IG4[m].t[:], IG4[m].t[:], S4[m].t[:], ALU.mult, [IG4[m], S4[m]], [IG4[m]])
            for m in M4:
                TT(v3(IG4[m])[:, :, 0], v3(IG4[m])[:, :, 0], TM4[m].t[:, 0:nseq], ALU.add, [IG4[m], TM4[m]], [IG4[m]])
            for m in M4:
                SCAN(HH4[m].t[:], AA4[m].t[:], IG4[m].t[:], [AA4[m], IG4[m]], [HH4[m]])
            for m in M4:
                CP(h0.t[:, m, :], v3(HH4[m])[:, :, L - 1], [HH4[m]], [h0])
            for m in M4:
                TT(OA.t[:, m, :], GA.t[:, m, :], HH4[m].t[:], ALU.mult, [GA, HH4[m]], [OA])
        if KSUB <= 3:
            A.off = base
            return
        if not P.dry:
            if kind == "s" or last:
                ns3 = nseq * 3
                SG0 = al([128, 4, ns3], F32, "cstage")
                CP(SG0.t[:].rearrange("p m (s j) -> p m s j", j=3), pA.t[:, :, :, L:L + 3], [pA], [SG0])
                bk = ps()
                for m in range(4):
                    TR(bk.t[0:ns3, m * 128:(m + 1) * 128], SG0.t[:, m, :], [SG0], [bk])
                CP(STG.t[0:ns3, 0:512], bk.t[0:ns3, 0:512], [bk], [STG])
                od = o_sconv.rearrange("s j f -> (s j) f") if kind == "s" else o_pconv
                DMA(od, STG.t[0:ns3, 0:512], STG, R=[STG])
                bk = ps()
                for m in range(4):
                    TR(bk.t[0:nseq, m * 128:(m + 1) * 128], h0.t[:, m, :], [h0], [bk])
                CP(STG.t[0:nseq, 0:512], bk.t[0:nseq, 0:512], [bk], [STG])
                od = o_sh if kind == "s" else o_ph
                DMA(od, STG.t[0:nseq, 0:512], STG, R=[STG])
            else:
                CP(PAH.t[:], pA.t[:, :, 0, L:L + 3], [pA], [PAH])

        if LIM <= 1:
            A.off = base
            return
        P.mark(f"{kind}{bi}:rwkv")
        A.off = m_l0
        P.barrier()
        nsub = TB // TBR
        ns = nseq if kind == "s" else 1
        Ls = L if kind == "s" else TBR
        nch = TBR // c
        gch = 2 if kind == "p" else 1
        PB = {nm: al([128, 4 * ns, 1 + Ls], F32, "PB" + nm) for nm in ("r", "k", "v")}
        PBWA = al([128, ns, 1 + Ls], F32, "PBWA"); PBG = al([128, ns, 1 + Ls], F32, "PBG")
        X = {nm: al([128, 4, TBR], F32, "X" + nm) for nm in ("r", "k", "v")}
        DD = al([128, 4, TBR], F32, "DD"); AA = al([128, 4, TBR], F32, "AA"); KK = al([128, 4, TBR], F32, "KK")
        SG = al([128, 4, TBR], F32, "SG"); CSG = al([128, 4, TBR], F32, "CSG"); TG = al([128, 4, TBR], F32, "TG")
        TWA = al([128, TBR], BF16, "TWA"); SXG = al([128, TBR], BF16, "SXG"); XWA = al([128, TBR], F32, "XWA")
        RS2 = al([128, TBR], F32, "RS2")
        SETS = []
        for _i in range(min(3, nsub)):
            SETS.append({nm: al([128, 4, TBR], BF16, nm) for nm in ("KAPG", "RG", "KDIV", "BDIV", "VB")})
            SETS[-1].update({"BON": al([128, 4, TBR], F32, "BON"), "GT": al([128, 4, TBR], F32, "GT"), "GC": al([128, 4, nch], F32, "GC")})
        OO = al([128, 4, TBR], F32, "OO"); TGP = al([128, 4, TBR], F32, "TGP"); RSP = al([128, TBR], F32, "RSP")
        CTS = [{nm: al([128, gch, 4, 64], BF16, nm) for nm in ("VT", "KDT", "BDT", "AKKT", "ARKT", "ARBT", "M_", "MT", "RT")}
               for _i in range(min(2, nsub))]
        WS_ = al([128, 4, 64], BF16, "WS_"); US = al([128, 4, 64], BF16, "US"); PB16 = al([128, 4, 64], BF16, "PB16")
        PBL = [P.buf("pb16_l0"), P.buf("pb16_l1")]
        LW = [P.buf("ws_l0"), P.buf("ws_l1")]; LU = [P.buf("us_l0"), P.buf("us_l1")]

        def rw_prep(sub):
            S = SETS[sub % len(SETS)]
            KAPG, RG, KDIV, BDIV, VB, BON, GT, GC = (S[k_] for k_ in ("KAPG", "RG", "KDIV", "BDIV", "VB", "BON", "GT", "GC"))
            n0 = sub * TBR
            sq0 = 0
            xrs = lambda k: xn.t[:, k, n0:n0 + TBR]

            def hist(pb, shx, nt):
                CP(pb.t[:].rearrange("p (m s) l -> p m s l", m=nt)[:, :, :, 0], shx.t[:, :, sq0:sq0 + ns], [shx], [pb])
            if not P.dry:
                for nm in ("r", "k", "v"):
                    hist(PB[nm], sh[nm], 4)
                hist(PBWA, sh["wa"], 1); hist(PBG, sh["g"], 1)

            def mk_epi(pb, nt):
                def epi(m, bk):
                    CP(pb.t[:].rearrange("p (m s) l -> p m s l", m=nt)[:, m, :, 1:1 + Ls],
                       bk.t[:, 0:TBR].rearrange("p (s l) -> p s l", l=Ls), [bk], [pb], e="act")
                return epi
            lin(w_in_ab, 0, 128, 8, 1024, 256, xrs, [xn], TBR, mk_epi(PB["r"], 4)); yield
            lin(w_in_ab, 0, 128, 8, 1280, 256, xrs, [xn], TBR, lambda m, bk: mk_epi(PB["r"], 4)(m + 2, bk)); yield
            lin(w_in_ab, 0, 128, 8, 1536, 256, xrs, [xn], TBR, mk_epi(PB["k"], 4)); yield
            lin(w_in_ab, 0, 128, 8, 1792, 256, xrs, [xn], TBR, lambda m, bk: mk_epi(PB["k"], 4)(m + 2, bk)); yield
            lin(w_in_ab, 0, 128, 8, 2048, 256, xrs, [xn], TBR, mk_epi(PB["v"], 4)); yield
            lin(w_in_ab, 0, 128, 8, 2304, 256, xrs, [xn], TBR, lambda m, bk: mk_epi(PB["v"], 4)(m + 2, bk)); yield

            def epi_l(m, bk):
                mk_epi(PBWA if m == 0 else PBG, 1)(0, bk)
            lin(w_in_ab, 0, 128, 8, 2560, 256, xrs, [xn], TBR, epi_l); yield
            if P.dry:
                return

            def save(pb, shx, nt):
                CP(shx.t[:, :, sq0:sq0 + ns], pb.t[:].rearrange("p (m s) l -> p m s l", m=nt)[:, :, :, Ls], [pb], [shx])
            for nm in ("r", "k", "v"):
                save(PB[nm], sh[nm], 4)
            save(PBWA, sh["wa"], 1); save(PBG, sh["g"], 1)
            yield
            v3s = lambda ap: ap.rearrange("p (s l) -> p s l", l=Ls)
            for nm in ("r", "k", "v"):
                pb = PB[nm]
                TT(DD.t[:].rearrange("p m (s l) -> p (m s) l", l=Ls), pb.t[:, :, 0:Ls], pb.t[:, :, 1:1 + Ls], ALU.subtract, [pb], [DD])
                TT(DD.t[:], DD.t[:], pbc(pcol("mu_" + nm, 0, 4), TBR), ALU.mult, [DD, PK], [DD])
                TT(X[nm].t[:].rearrange("p m (s l) -> p (m s) l", l=Ls), DD.t[:].rearrange("p m (s l) -> p (m s) l", l=Ls), pb.t[:, :, 1:1 + Ls], ALU.add, [DD, pb], [X[nm]])
                yield
            for pb, mu, dst in ((PBWA, "mu_wa", XWA), (PBG, "mu_g", RS2)):
                TT(v3s(DD.t[:, 0, :]), pb.t[:, :, 0:Ls], pb.t[:, :, 1:1 + Ls], ALU.subtract, [pb], [DD])
                STT(v3s(dst.t[:]), v3s(DD.t[:, 0, :]), pcol(mu), pb.t[:, :, 1:1 + Ls], ALU.mult, ALU.add, [DD, pb, PK], [dst])
            ACT(TWA.t[0:64, :], XWA.t[0:64, :], AF.Tanh, [XWA], [TWA])
            ACT(TWA.t[64:128, :], XWA.t[64:128, :], AF.Copy, [XWA], [TWA])
            ACT(SXG.t[:], RS2.t[:], AF.Sigmoid, [RS2], [SXG])
            yield
            b3v = lambda bk_: bk_.t[:, 0:4 * TBR].rearrange("p (m t) -> p m t", m=4)
            bk = ps()
            for m in range(4):
                MM(bk.t[:, m * TBR:(m + 1) * TBR], W2.t[0:64, m * 128:(m + 1) * 128], TWA.t[0:64, :], True, True, [W2, TWA], [bk])
            TT(SG.t[:], b3v(bk), pbc(pcol("w0", 0, 4), TBR), ALU.add, [bk, PK], [SG])
            ACT(SG.t[:], SG.t[:], AF.Sigmoid, [SG], [SG])
            yield
            bk = ps()
            for m in range(4):
                MM(bk.t[:, m * TBR:(m + 1) * TBR], A2.t[64:128, m * 128:(m + 1) * 128], TWA.t[64:128, :], True, True, [A2, TWA], [bk])
            TT(AA.t[:], b3v(bk), pbc(pcol("a0", 0, 4), TBR), ALU.add, [bk, PK], [AA])
            ACT(AA.t[:], AA.t[:], AF.Sigmoid, [AA], [AA])
            yield
            bk = ps()
            for m in range(4):
                MM(bk.t[:, m * TBR:(m + 1) * TBR], G2.t[:, m * 128:(m + 1) * 128], SXG.t[:], True, True, [G2, SXG], [bk])
            CP(GT.t[:], b3v(bk), [bk], [GT], e="act")
            yield
            fl = lambda t: t.t[:].rearrange("p m t -> p (m t)")
            SCAN(fl(CSG), RM.t[:, 0:4 * TBR], fl(SG), [RM, SG], [CSG])
            TT(KK.t[:], X["k"].t[:], pbc(pcol("kkb", 0, 4), TBR), ALU.mult, [X["k"], PK], [KK])
            TT(DD.t[:], KK.t[:], KK.t[:], ALU.mult, [KK], [DD])
            yield
            bk = ps()
            for m in range(4):
                MM(bk.t[:, m * TBR:(m + 1) * TBR], BONES, DD.t[:, m, :], True, True, CB + [DD], [bk])
            TS(DD.t[:], b3v(bk), 1e-24, None, ALU.max, None, [bk], [DD])
            ACT(DD.t[:], DD.t[:], AF.Sqrt, [DD], [DD])
            RCP(DD.t[:], DD.t[:], [DD], [DD])
            TT(KK.t[:], KK.t[:], DD.t[:], ALU.mult, [KK, DD], [KK])
            yield
            TT(DD.t[:], AA.t[:], pbc(pcol("ka", 0, 4), TBR), ALU.mult, [AA, PK], [DD])
            TT(DD.t[:], DD.t[:], pbc(DER.t[:, 24:28], TBR), ALU.add, [DD, DER], [DD])
            TT(X["k"].t[:], X["k"].t[:], DD.t[:], ALU.mult, [X["k"], DD], [X["k"]])
            TT(DD.t[:], X["r"].t[:], X["k"].t[:], ALU.mult, [X["r"], X["k"]], [DD])
            yield
            bk = ps()
            for m in range(4):
                MM(bk.t[:, m * TBR:(m + 1) * TBR], RKM.t[:, m, :], DD.t[:, m, :], True, True, [RKM, DD], [bk])
            TT(BON.t[:], b3v(bk), X["v"].t[:], ALU.mult, [bk, X["v"]], [BON])
            CP(VB.t[:], X["v"].t[:], [X["v"]], [VB], e="act")
            yield
            TT(DD.t[:], KK.t[:], AA.t[:], ALU.mult, [KK, AA], [DD])
            ACT(TG.t[:], CSG.t[:], AF.Exp, [CSG], [TG], scale=-EW)
            TT(RG.t[:], X["r"].t[:], TG.t[:], ALU.mult, [X["r"], TG], [RG])
            CP(GC.t[:], TG.t[:].rearrange("p m (j c) -> p m j c", c=c)[:, :, :, c - 1], [TG], [GC])
            yield
            ACT(TG.t[:], CSG.t[:], AF.Exp, [CSG], [TG], scale=EW)
            TT(KDIV.t[:], X["k"].t[:], TG.t[:], ALU.mult, [X["k"], TG], [KDIV])
            TT(BDIV.t[:], DD.t[:], TG.t[:], ALU.mult, [DD, TG], [BDIV])
            yield
            TT(CSG.t[:], CSG.t[:], SG.t[:], ALU.subtract, [CSG, SG], [CSG])
            ACT(TG.t[:], CSG.t[:], AF.Exp, [CSG], [TG], scale=-EW)
            TT(KAPG.t[:], KK.t[:], TG.t[:], ALU.mult, [KK, TG], [KAPG])
            yield

        def pump(gen, n=1):
            if gen is None:
                return
            for _ in range(n):
                try:
                    next(gen)
                except StopIteration:
                    return

        def rw_an(sub, gi_):
            S = SETS[sub % len(SETS)]
            KAPG, RG, KDIV, BDIV, VB, BON, GT, GC = (S[k_] for k_ in ("KAPG", "RG", "KDIV", "BDIV", "VB", "BON", "GT", "GC"))
            C_ = CTS[sub % len(CTS)]
            VT, KDT, BDT, AKKT, ARKT, ARBT, M_, MT, RT = (C_[k_] for k_ in ("VT", "KDT", "BDT", "AKKT", "ARKT", "ARBT", "M_", "MT", "RT"))
            n0 = sub * TBR
            sq0 = 0
            for j in range(gi_ * gch, (gi_ + 1) * gch):
                jl = j % gch
                cs = slice(j * c, (j + 1) * c)
                for par in range(2):
                    P_ = slice(par * 64, par * 64 + 64); Pc = slice(par * 64, par * 64 + c)
                    for src, dst, neg in ((VB, VT, False), (KDIV, KDT, False), (BDIV, BDT, True)):
                        bk = ps()
                        for i in range(4):
                            MM(bk.t[Pc, i * 64:(i + 1) * 64], src.t[P_, i, cs], IDENTB.t[P_, P_], True, True, [src, IDENTB], [bk])
                        o_ = dst.t[Pc, jl, :, :].rearrange("p m f -> p (m f)")
                        if neg:
                            ACT(o_, bk.t[Pc, 0:256], AF.Copy, [bk], [dst], scale=-1.0)
                        else:
                            ACT(o_, bk.t[Pc, 0:256], AF.Copy, [bk], [dst])
                    yield
                    specs = ((KDIV, KAPG, AKKT, MS), (BDIV, KAPG, MT, MS), (KAPG, BDIV, M_, ML), (KDIV, RG, ARKT, MI), (BDIV, RG, ARBT, MIN_))
                    for lt, rt, dst, msk in specs:
                        bk = ps()
                        for i in range(4):
                            MM(bk.t[Pc, i * c:(i + 1) * c], lt.t[P_, i, cs], rt.t[P_, i, cs], True, True, [lt, rt], [bk])
                        TT(dst.t[Pc, jl, :, 0:c], bk.t[Pc, 0:4 * c].rearrange("p (h t) -> p h t", h=4), msk[Pc, slice(0, 4), slice(0, c)], ALU.mult, [bk] + CB, [dst])
                    TT(RT.t[Pc, jl, :, 0:c], IDM[Pc, slice(0, 4), slice(0, c)], MT.t[Pc, jl, :, 0:c], ALU.subtract, CB + [MT], [RT])
                    yield
            for lv in range(nlev if KR > 1 else 0):
                for j in range(gi_ * gch, (gi_ + 1) * gch):
                    jl = j % gch
                    for par in range(2):
                        Pc = slice(par * 64, par * 64 + c)
                        v4 = lambda b_: b_.t[Pc, 0:4 * c].rearrange("p (h t) -> p h t", h=4)
                        bka = ps()
                        for i in range(4):
                            MM(bka.t[Pc, i * c:(i + 1) * c], MT.t[Pc, jl, i, 0:c], M_.t[Pc, jl, i, 0:c], True, True, [MT, M_], [bka])
                        if lv < nlev - 1:
                            bkb = ps()
                            for i in range(4):
                                MM(bkb.t[Pc, i * c:(i + 1) * c], M_.t[Pc, jl, i, 0:c], MT.t[Pc, jl, i, 0:c], True, True, [MT, M_], [bkb])
                        CP(M_.t[Pc, jl, :, 0:c], v4(bka), [bka], [M_], e="act")
                        if lv < nlev - 1:
                            CP(MT.t[Pc, jl, :, 0:c], v4(bkb), [bkb], [MT], e="act")
                        bkc = ps()
                        for i in range(4):
                            MM(bkc.t[Pc, i * c:(i + 1) * c], IDENTB.t[Pc, Pc], RT.t[Pc, jl, i, 0:c], True, False, [IDENTB, RT], [bkc])
                            MM(bkc.t[Pc, i * c:(i + 1) * c], M_.t[Pc, jl, i, 0:c], RT.t[Pc, jl, i, 0:c], False, True, [M_, RT], [bkc])
                        if lv % 2 == 0:
                            CP(RT.t[Pc, jl, :, 0:c], v4(bkc), [bkc], [RT], e="act")
                        else:
                            CP(RT.t[Pc, jl, :, 0:c], v4(bkc), [bkc], [RT])
                    yield

        def rw_seq(sub, gi_, gen):
            S = SETS[sub % len(SETS)]
            KAPG, RG, KDIV, BDIV, VB, BON, GT, GC = (S[k_] for k_ in ("KAPG", "RG", "KDIV", "BDIV", "VB", "BON", "GT", "GC"))
            C_ = CTS[sub % len(CTS)]
            VT, KDT, BDT, AKKT, ARKT, ARBT, M_, MT, RT = (C_[k_] for k_ in ("VT", "KDT", "BDT", "AKKT", "ARKT", "ARBT", "M_", "MT", "RT"))
            n0 = sub * TBR
            sq0 = 0
            for j in range(gi_ * gch, (gi_ + 1) * gch if KR > 2 else 0):
                jl = j % gch
                cs = slice(j * c, (j + 1) * c)
                if kind == "s":
                    sq = sq0 + j
                    si = SIN[sq % 2]
                    for sq2 in [sq]:
                        if sq2 < nseq:
                            si2 = SIN[sq2 % 2]
                            DMA(si2.t[:].rearrange("v m (j k) -> v m j k", j=2), st_rS[sq2].rearrange("(m j) v k -> v m j k", j=2), si2, W=[si2])
                    bk = ps()
                    for m in range(4):
                        TR(bk.t[:, m * 64:(m + 1) * 64], si.t[:, m, :], [si], [bk])
                    CP(PB16.t[:], bk.t[:, 0:256].rearrange("p (m v) -> p m v", m=4), [bk], PBL)
                    CP(PST.t[:], bk.t[:, 0:256].rearrange("p (m v) -> p m v", m=4), [bk], [PST])
                if kind == "p" and sub == 0 and j == 0:
                    CP(PB16.t[:], PST.t[:], [PST], PBL, e="act")
                LP = [(slice(par * 64, par * 64 + 64), slice(par * 64, par * 64 + c)) for par in range(2)]
                for par, (P_, Pc) in enumerate(LP):
                    bkw = ps()
                    for i in range(4):
                        MM(bkw.t[Pc, i * 64:(i + 1) * 64], KAPG.t[P_, i, cs], PB16.t[P_, i, :], True, False, [KAPG, PBL[par]], [bkw])
                        MM(bkw.t[Pc, i * 64:(i + 1) * 64], AKKT.t[Pc, jl, i, 0:c], VT.t[Pc, jl, i, :], False, True, [AKKT, VT], [bkw])
                    CP(WS_.t[Pc].rearrange("p h v -> p (h v)"), bkw.t[Pc, 0:256], [bkw], [LW[par]])
                    pump(gen, PUMPN)
                for par, (P_, Pc) in enumerate(LP):
                    bku = ps()
                    for i in range(4):
                        MM(bku.t[Pc, i * 64:(i + 1) * 64], RT.t[Pc, jl, i, 0:c], WS_.t[Pc, i, :], True, True, [RT, LW[par]], [bku])
                    CP(US.t[Pc].rearrange("p h v -> p (h v)"), bku.t[Pc, 0:256], [bku], [LU[par]], e="act")
                    pump(gen, PUMPN)
                for par, (P_, Pc) in enumerate(LP):
                    bko = ps(); bkp = ps()
                    for i in range(4):
                        oo = bko.t[P_, i * c:(i + 1) * c]
                        MM(oo, PB16.t[P_, i, :], RG.t[P_, i, cs], True, False, [PBL[par], RG], [bko])
                        MM(oo, VT.t[Pc, jl, i, :], ARKT.t[Pc, jl, i, 0:c], False, False, [VT, ARKT], [bko])
                        MM(oo, US.t[Pc, i, :], ARBT.t[Pc, jl, i, 0:c], False, True, [LU[par], ARBT], [bko])
                        pp = bkp.t[P_, i * 64:(i + 1) * 64]
                        MM(pp, IDENT[P_, P_], PST.t[P_, i, :], True, False, CB + [PST], [bkp])
                        MM(pp, KDT.t[Pc, jl, i, :], VT.t[Pc, jl, i, :], False, False, [KDT, VT], [bkp])
                        MM(pp, BDT.t[Pc, jl, i, :], US.t[Pc, i, :], False, True, [BDT, LU[par]], [bkp])
                    CP(OO.t[P_, :, cs], bko.t[P_, 0:4 * c].rearrange("p (m t) -> p m t", m=4), [bko], [OO], e="act")
                    if kind == "p":
                        TT(PB16.t[P_, :, :], bkp.t[P_, 0:256].rearrange("p (m v) -> p m v", m=4), GC.t[P_, :, j:j + 1].to_broadcast([64, 4, 64]), ALU.mult, [bkp, GC], [PBL[par]])
                    TT(PST.t[P_, :, :], bkp.t[P_, 0:256].rearrange("p (m v) -> p m v", m=4), GC.t[P_, :, j:j + 1].to_broadcast([64, 4, 64]), ALU.mult, [bkp, GC], [PST])
                    pump(gen, PUMPN)
                if kind == "s" or (last and sub == nsub - 1 and j == nch - 1):
                    so = SOUT[j % 2]
                    bk = ps()
                    for m in range(4):
                        TR(bk.t[0:64, m * 128:(m + 1) * 128], PST.t[:, m, :], [PST], [bk])
                    CP(so.t[:].rearrange("v m f -> v (m f)"), bk.t[0:64, :], [bk], [so])
                    od = o_srS[sq0 + j] if kind == "s" else o_prS
                    DMA(od.rearrange("(m j) v k -> v m j k", j=2), so.t[:].rearrange("v m (j k) -> v m j k", j=2), so, R=[so])

        def rw_post(sub, gen):
            S = SETS[sub % len(SETS)]
            KAPG, RG, KDIV, BDIV, VB, BON, GT, GC = (S[k_] for k_ in ("KAPG", "RG", "KDIV", "BDIV", "VB", "BON", "GT", "GC"))
            C_ = CTS[sub % len(CTS)]
            VT, KDT, BDT, AKKT, ARKT, ARBT, M_, MT, RT = (C_[k_] for k_ in ("VT", "KDT", "BDT", "AKKT", "ARKT", "ARBT", "M_", "MT", "RT"))
            n0 = sub * TBR
            sq0 = 0
            P.mark(f"{kind}{bi}:rwkv_post{sub}")
            b3v = lambda bk_: bk_.t[:, 0:4 * TBR].rearrange("p (m t) -> p m t", m=4)
            bk = ps()
            for m in range(4):
                MM(bk.t[:, m * TBR:(m + 1) * TBR], BMEAN.t[:], OO.t[:, m, :], True, True, [BMEAN, OO], [bk])
            TT(OO.t[:], OO.t[:], b3v(bk), ALU.subtract, [OO, bk], [OO])
            TT(TGP.t[:], OO.t[:], OO.t[:], ALU.mult, [OO], [TGP])
            pump(gen, PUMPN)
            bk = ps()
            for m in range(4):
                MM(bk.t[:, m * TBR:(m + 1) * TBR], BMEAN.t[:], TGP.t[:, m, :], True, True, [BMEAN, TGP], [bk])
            ACT(TGP.t[:], b3v(bk), AF.Sqrt, [bk], [TGP], bias=64e-5)
            RCP(TGP.t[:], TGP.t[:], [TGP], [TGP])
            TT(OO.t[:], OO.t[:], TGP.t[:], ALU.mult, [OO, TGP], [OO])
            pump(gen, PUMPN)
            TT(OO.t[:], OO.t[:], pbc(pcol("lnx_w", 0, 4), TBR), ALU.mult, [OO, PK], [OO])
            TT(OO.t[:], OO.t[:], pbc(pcol("lnx_b", 0, 4), TBR), ALU.add, [OO, PK], [OO])
            pump(gen, PUMPN)
            TT(OO.t[:], OO.t[:], BON.t[:], ALU.add, [OO, BON], [OO])
            TT(OB.t[:, :, n0:n0 + TBR], OO.t[:], GT.t[:], ALU.mult, [OO, GT], [OB])

        class Multi:
            def __init__(self, gens):
                self.gens = [g for g in gens if g is not None]

            def __next__(self):
                alive = False
                for g in list(self.gens):
                    try:
                        next(g); alive = True
                    except StopIteration:
                        self.gens.remove(g)
                if not alive:
                    raise StopIteration
        ngrp = nch // gch
        if P.dry:
            for sub in range(nsub):
                pump(rw_prep(sub), 10 ** 6)
        elif kind == "s":
            pump(rw_prep(0), 10 ** 6)
            P.mark(f"{kind}{bi}:rwkv_chunks0")
            for gi_ in range(ngrp):
                pump(rw_an(0, gi_), 10 ** 6)
                rw_seq(0, gi_, None)
            rw_post(0, None)
        else:
            pump(rw_prep(0), 10 ** 6)
            pump(Multi([rw_an(0, 0), rw_prep(1) if nsub > 1 else None]), 10 ** 6)
            for sub in range(nsub):
                P.mark(f"{kind}{bi}:rwkv_chunks{sub}")
                gen = Multi([rw_an(sub + 1, 0) if sub + 1 < nsub else None, rw_prep(sub + 2) if sub + 2 < nsub else None])
                rw_seq(sub, 0, gen)
                rw_post(sub, gen)
                pump(gen, 10 ** 6)
        if not P.dry and (kind == "s" or last):
            srcs = [(nm, m) for nm, nt in (("r", 4), ("k", 4), ("v", 4), ("wa", 1), ("g", 1)) for m in range(nt)]
            for g0 in range(0, 14, 4):
                grp = srcs[g0:g0 + 4]
                bk = ps()
                for q, (nm, m) in enumerate(grp):
                    TR(bk.t[0:nseq, q * 128:(q + 1) * 128], sh[nm].t[:, m, :], [sh[nm]], [bk])
                CP(STG.t[0:nseq, 0:len(grp) * 128], bk.t[0:nseq, 0:len(grp) * 128], [bk], [STG])
                od = o_sshift if kind == "s" else o_pshift
                DMA(od[:, g0 * 128:(g0 + len(grp)) * 128], STG.t[0:nseq, 0:len(grp) * 128], STG, R=[STG])

        if LIM <= 2:
            A.off = base
            return
        P.mark(f"{kind}{bi}:wout")
        A.off = m_l0
        P.barrier()

        def oab(k):
            return OA.t[:, k, :] if k < 4 else OB.t[:, k - 4, :]

        ST = {"bk": None, "pend": []}
        SQB = [P.buf(f"sqb{k}") for k in range(8)]

        def stat_mm(k):
            MM(ST["bk"].t[:, 0:TB], ONESB.t[:], xn.t[:, k, 0:TB], k == 0, k == 7, [ONESB, SQB[k]], [ST["bk"]])

        def stat_chunk(k):
            if P.dry or not FUSE_STAT:
                return
            if k == 0:
                ST["bk"] = ps(); PS_RES[0] = psb.index(ST["bk"]); ST["pend"] = []
            ACT(xn.t[:, k, 0:TB], xT.t[:, k, 0:TB], AF.Square, [xT], [SQB[k]] + ([xn] if k == 0 else []))
            ST["pend"].append(k)
            if len(ST["pend"]) > 2:
                stat_mm(ST["pend"].pop(0))

        def rms_finish(gname, dst, mark):
            if P.dry or not FUSE_STAT:
                rmsnorm(xT, gname, dst, TB, mark)
                return
            while ST["pend"]:
                stat_mm(ST["pend"].pop(0))
            A.off = mark
            RS = al([128, TB], F32, "rs")
            bk = ST["bk"]
            ACT(RS.t[:], bk.t[:, 0:TB], AF.Sqrt, [bk], [RS], bias=1e-6, scale=1.0 / 1024.0)
            PS_RES[0] = None
            RCP(RS.t[:], RS.t[:], [RS], [RS])
            for k in range(8):
                STT(dst.t[:, k, 0:TB], xT.t[:, k, 0:TB], pcol(gname, k), RS.t[:], ALU.mult, ALU.mult, [xT, RS, PK], [dst, SQB[k]])
            P.barrier()

        def mk_res(cb):
            def epi(m, bk):
                TT(xT.t[:, cb + m, 0:TB], bk.t[:, 0:TB], xT.t[:, cb + m, 0:TB], ALU.add, [bk, xT], [xT])
                stat_chunk(cb + m)
            return epi
        lin(w_out_ab, 0, 128, 8, 0, 512, oab, [OA, OB], TB, mk_res(0))
        lin(w_out_ab, 0, 128, 8, 512, 512, oab, [OA, OB], TB, mk_res(4))

        def ffn(l):
            P.mark(f"{kind}{bi}:ffn{l}")
            A.off = base
            P.barrier()
            rms_finish(f"ln_ffn{l}", xn, base)
            A.off = base
            HH = al([128, 22, TB], BF16, "HH"); SGt = al([128, TB], F32, "SGt")
            for gi in range(11):
                c0 = gi * 256; ncol = 256
                sg_, sgb = WSTR.get(ffn_gate[l], 0, 128, 8, c0, ncol)
                su_, sub_ = WSTR.get(ffn_up[l], 0, 128, 8, c0, ncol)
                if P.dry:
                    continue
                for m in range(ncol // 128):
                    bg = ps(); bu = ps()
                    for k in range(8):
                        MM(bg.t[:, 0:TB], sg_[:, k, m * 128:(m + 1) * 128], xn.t[:, k, 0:TB], k == 0, k == 7, [sgb, xn], [bg])
                    for k in range(8):
                        MM(bu.t[:, 0:TB], su_[:, k, m * 128:(m + 1) * 128], xn.t[:, k, 0:TB], k == 0, k == 7, [sub_, xn], [bu])
                    ACT(SGt.t[:], bg.t[:, 0:TB], AF.Silu, [bg], [SGt])
                    TT(HH.t[:, gi * 2 + m, :], SGt.t[:], bu.t[:, 0:TB], ALU.mult, [SGt, bu], [HH])
            for m in range(8):
                lin(ffn_down[l], 0, 128, 22, m * 128, 128, lambda k: HH.t[:, k, :], [HH], TB, mk_res(m))
        ffn(0)

        if LIM <= 3:
            A.off = base
            return
        P.mark(f"{kind}{bi}:hgrn")
        A.off = base
        P.barrier()
        rms_finish("ln_mix1", xn, base)
        A.off = base
        OH = OHP; SGG = al([128, 8, TB], BF16, "SGG")
        m_l1 = A.off
        nchb = TB // c
        A.off = m_l1
        P.barrier()
        HSETS = [{"QD": al([128, 4, TB], BF16, "QD"), "KD": al([128, 4, TB], BF16, "KD"),
                  "IT": al([64, nchb, 512], BF16, "IT"), "EBL": al([128, 4, nchb], F32, "EBL")} for _i in range(2)]
        SB162 = [al([128, 4, 128], BF16, "SB16") for _i in range(2)]
        SLO2 = [al([128, 4, 128], BF16, "SLO") for _i in range(2)]
        m_q = A.off
        Q = al([128, 4, TB], F32, "Q"); FF = al([128, 4, TB], F32, "FF"); BC = al([128, 4, TB], F32, "BC")
        EE = al([128, 4, TB], F32, "EE")
        nkb = nchb if kind == "p" else 1
        KDTh2 = [None, None]; ATT2 = [None, None]
        RMh = RM64 if c == 64 else RM4

        def hg_prep(half):
            S = HSETS[half]
            QD, KD, IT, EBL = S["QD"], S["KD"], S["IT"], S["EBL"]
            hh0 = half * 4

            def epi_q(m, bk):
                ACT(Q.t[:, m, :], bk.t[:, 0:TB], AF.Silu, [bk], [Q])
            lin(w_in_c, 0, 128, 8, half * 512, 256, xr, [xn], TB, epi_q); yield
            lin(w_in_c, 0, 128, 8, half * 512 + 256, 256, xr, [xn], TB, lambda m, bk: epi_q(m + 2, bk)); yield

            def epi_f(m, bk):
                ACT(FF.t[:, m, :], bk.t[:, 0:TB], AF.Sigmoid, [bk], [FF])
                TS(FF.t[:, m, :], FF.t[:, m, :], OML(hh0 + m), LB(hh0 + m), ALU.mult, ALU.add, [FF, DER], [FF])
            lin(w_in_c, 0, 128, 8, 1024 + half * 512, 256, xr, [xn], TB, epi_f); yield
            lin(w_in_c, 0, 128, 8, 1024 + half * 512 + 256, 256, xr, [xn], TB, lambda m, bk: epi_f(m + 2, bk)); yield
            for u in range(2):
                slot, sb = WSTR.get(w_in_c, 0, 128, 8, 2048 + half * 512 + u * 256, 256)
                if not P.dry:
                    for j in range(nchb):
                        bk = ps()
                        for k in range(8):
                            MM(bk.t[0:c, 0:256], xn.t[:, k, j * c:(j + 1) * c], slot[:, k, :], k == 0, k == 7, [sb, xn], [bk])
                        CP(IT.t[0:c, j, u * 256:(u + 1) * 256], bk.t[0:c, 0:256], [bk], [IT], e="act")
                        if j % 2 == 1:
                            yield

            def epi_gg(m, bk):
                ACT(SGG.t[:, hh0 + m, 0:TB], bk.t[:, 0:TB], AF.Silu, [bk], [SGG])
            lin(w_in_c, 0, 128, 8, 3072 + half * 512, 256, xr, [xn], TB, epi_gg); yield
            lin(w_in_c, 0, 128, 8, 3072 + half * 512 + 256, 256, xr, [xn], TB, lambda m, bk: epi_gg(m + 2, bk)); yield
            if P.dry:
                return
            fl = lambda t: t.t[:].rearrange("p m t -> p (m t)")
            ACT(EE.t[:], FF.t[:], AF.Ln, [FF], [EE])
            TS(FF.t[:], FF.t[:], -1.0, 1.0, ALU.mult, ALU.add, [FF], [FF])
            yield
            SCAN(fl(BC), RMh.t[:, 0:4 * TB], fl(EE), [RMh, EE], [BC])
            CP(EBL.t[:], BC.t[:].rearrange("p m (j c) -> p m j c", c=c)[:, :, :, c - 1], [BC], [EBL])
            ACT(EBL.t[:], EBL.t[:], AF.Exp, [EBL], [EBL])
            yield
            ACT(EE.t[:], BC.t[:], AF.Exp, [BC], [EE])
            TT(QD.t[:], Q.t[:], EE.t[:], ALU.mult, [Q, EE], [QD])
            yield
            TS(BC.t[:], BC.t[:], -1.0, 80.0, ALU.mult, ALU.min, [BC], [BC])
            ACT(EE.t[:], BC.t[:], AF.Exp, [BC], [EE])
            TT(KD.t[:], FF.t[:], EE.t[:], ALU.mult, [FF, EE], [KD])
            yield

        def hg_pre(half, j):
            S = HSETS[half]
            QD, KD = S["QD"], S["KD"]
            KDTh, ATT = KDTh2[half], ATT2[half]
            cs = slice(j * c, (j + 1) * c)
            bk = ps()
            for m in range(4):
                MM(bk.t[0:c, m * 128:(m + 1) * 128], KD.t[:, m, cs], IDENTB.t[:], True, True, [KD, IDENTB], [bk])
            CP(KDTh.t[0:c, j % nkb].rearrange("p m f -> p (m f)"), bk.t[0:c, :], [bk], [KDTh], e="act")
            bk = ps()
            for m in range(4):
                MM(bk.t[0:c, m * c:(m + 1) * c], KD.t[:, m, cs], QD.t[:, m, cs], True, True, [KD, QD], [bk])
            TT(ATT.t[0:c, j % nkb, :, 0:c], bk.t[0:c, 0:4 * c].rearrange("p (m t) -> p m t", m=4), MI[slice(0, c), slice(0, 4), slice(0, c)], ALU.mult, [bk] + CB, [ATT])

        def hg_chunk(half, j):
            S = HSETS[half]
            QD, KD, IT, EBL = S["QD"], S["KD"], S["IT"], S["EBL"]
            KDTh, ATT, SB16, SLO = KDTh2[half], ATT2[half], SB162[half], SLO2[half]
            hh0 = half * 4
            cs = slice(j * c, (j + 1) * c)
            if kind == "s":
                q_ = 2 * j + half
                St = HSB[q_ % 3]
                if q_ == 0:
                    for qq in (0, 1):
                        DMA(HSB[qq % 3].t[:], st_hS[qq // 2, (qq % 2) * 4:(qq % 2) * 4 + 4].rearrange("h k v -> k h v"), HSB[qq % 3], W=[HSB[qq % 3]])
                qq = q_ + 2
                if qq < 2 * nchb:
                    DMA(HSB[qq % 3].t[:], st_hS[qq // 2, (qq % 2) * 4:(qq % 2) * 4 + 4].rearrange("h k v -> k h v"), HSB[qq % 3], W=[HSB[qq % 3]])
                Sv = lambda m: St.t[:, m, :]
                CP(SB16.t[:], St.t[:], [St], [SB16], e="act")
            else:
                St = HS
                Sv = lambda m: HS.t[:, hh0 + m, :]
                if j == 0:
                    CP(SB16.t[:], HS.t[:, hh0:hh0 + 4, :], [HS], [SB16], e="act")
                    TT(SLO.t[:], HS.t[:, hh0:hh0 + 4, :], SB16.t[:], ALU.subtract, [HS, SB16], [SLO])
            bko = ps(); bks = ps()
            for m in range(4):
                MM(bko.t[:, m * c:(m + 1) * c], IT.t[0:c, j, m * 128:(m + 1) * 128], ATT.t[0:c, j % nkb, m, 0:c], True, False, [IT, ATT], [bko])
                MM(bko.t[:, m * c:(m + 1) * c], SB16.t[:, m, :], QD.t[:, m, cs], False, True, [SB16, QD], [bko])
                if kind == "p":
                    MM(bks.t[:, m * 128:(m + 1) * 128], IDENTB.t[:], SB16.t[:, m, :], True, False, [IDENTB, SB16], [bks])
                    MM(bks.t[:, m * 128:(m + 1) * 128], IDENTB.t[:], SLO.t[:, m, :], False, False, [IDENTB, SLO], [bks])
                MM(bks.t[:, m * 128:(m + 1) * 128], KDTh.t[0:c, j % nkb, m, :], IT.t[0:c, j, m * 128:(m + 1) * 128], kind == "s", True, [KDTh, IT], [bks])
            CP(OH.t[:, hh0:hh0 + 4, cs], bko.t[:, 0:4 * c].rearrange("p (m t) -> p m t", m=4), [bko], [OH], e="act")
            nxt = kind == "p" and j < nchb - 1
            ebc = EBL.t[:, :, j:j + 1].to_broadcast([128, 4, 128])
            b3 = bks.t[:].rearrange("p (m v) -> p m v", m=4)
            if nxt:
                TT(SB16.t[:], b3, ebc, ALU.mult, [bks, EBL], [SB16])
            if kind == "s":
                TT(St.t[:], b3, St.t[:], ALU.add, [bks, St], [St])
                TT(St.t[:], St.t[:], ebc, ALU.mult, [St, EBL], [St])
            else:
                TT(HS.t[:, hh0:hh0 + 4, :], b3, ebc, ALU.mult, [bks, EBL], [St])
                if nxt:
                    TT(SLO.t[:], HS.t[:, hh0:hh0 + 4, :], SB16.t[:], ALU.subtract, [HS, SB16], [SLO])
            if kind == "s":
                DMA(o_shS[j, hh0:hh0 + 4].rearrange("h k v -> k h v"), St.t[:], St, R=[St])

        if kind == "p":
            pump(hg_prep(0), 10 ** 6)
            g1 = hg_prep(1)
            pump(g1, 6 + nchb)
            KDTh2[0] = A.alias(SIN[0], [64, nkb, 4, 128], BF16, "KDTh"); ATT2[0] = al([64, nkb, 4, 64], BF16, "ATT")
            if not P.dry:
                for j in range(nchb):
                    hg_pre(0, j)
                    pump(g1, 1)
            pump(g1, 10 ** 6)
            A.off = m_q
            P.barrier()
            KDTh2[1] = al([64, nkb, 4, 128], BF16, "KDTh"); ATT2[1] = al([64, nkb, 4, 64], BF16, "ATT")
            if not P.dry:
                for j in range(nchb):
                    hg_pre(1, j)
        else:
            for half in range(2):
                pump(hg_prep(half), 10 ** 6)
            A.off = m_q
            P.barrier()
            KDTh2[:] = [al([64, nkb, 4, 128], BF16, "KDTh") for _i in range(2)]; ATT2[:] = [al([64, nkb, 4, 64], BF16, "ATT") for _i in range(2)]
        if not P.dry:
            for j in range(nchb):
                for half in range(2):
                    if kind == "s":
                        hg_pre(half, j)
                    hg_chunk(half, j)
            if last:
                DMA(o_phS.rearrange("h k v -> k h v"), HS.t[:], HS, R=[HS])
        A.off = m_l1
        P.barrier()
        P.mark(f"{kind}{bi}:hgrn_out")
        XN2 = al([128, 8, TB], BF16, "XN2")
        m2 = A.off
        rmsnorm(OH, "gn", XN2, TB, m2)
        if not P.dry:
            TT(XN2.t[:], XN2.t[:], SGG.t[:, :, 0:TB], ALU.mult, [XN2, SGG], [XN2])
        lin(w_out_c, 0, 128, 8, 0, 512, lambda k: XN2.t[:, k, :], [XN2], TB, mk_res(0))
        lin(w_out_c, 0, 128, 8, 512, 512, lambda k: XN2.t[:, k, :], [XN2], TB, mk_res(4))
        ffn(1)

        if LIM <= 4:
            A.off = base
            return
        P.mark(f"{kind}{bi}:final")
        A.off = base
        P.barrier()
        XO = al([128, 8, TB], F32, "XO")
        rms_finish("ln_final", XO, A.off)
        if not P.dry:
            for j in range(ntt):
                for g in range(2):
                    bk = ps()
                    for q in range(4):
                        TR(bk.t[0:rows, q * 128:(q + 1) * 128], XO.t[:, g * 4 + q, j * 128:j * 128 + rows], [XO], [bk])
                    CP(XST.t[0:rows, j, g * 512:(g + 1) * 512], bk.t[0:rows, :], [bk], [XST], e="act")
            if kind == "p":
                DMA(yd.rearrange("(j p) d -> p j d", p=128), XST.t[:], XST, R=[XST])
            else:
                DMA(yd, XST.t[0:64, 0, :], XST, R=[XST])
        A.off = base

    def program():
        if not SAMPLE_LAST and not os.environ.get('KSKIPS'):
            run_block("s", 0)
        for bi in range(NBLK):
            run_block("p", bi)
        if SAMPLE_LAST and not os.environ.get('KSKIPS'):
            run_block("s", 0)

    P.dry = True
    program()
    print("arena hi", A.hi)
    P.dry = False
    WSTR.reset()
    setup()
    program()
    outs = [xtok, xT, STG, SOUT[0], SOUT[1], HS] + HSB
    P.mark("end")
    if os.environ.get("KMARKS"):
        import json
        json.dump(P.marks, open(os.environ["KMARKS"], "w"))
    P.finish([o.b for o in outs])
    P.emit()
    return nc


PK_OFF = {}
PK_N = 0


def _pk_layout():
    global PK_N
    off = 0
    for name, n in (("ln_mix0", 8), ("ln_mix1", 8), ("ln_ffn0", 8), ("ln_ffn1", 8), ("ln_final", 8), ("gn", 8),
                    ("conv_w", 16), ("conv_b", 4), ("gr_b", 4), ("gi_b", 4), ("lam", 4),
                    ("mu_r", 4), ("mu_k", 4), ("mu_v", 4), ("mu_wa", 1), ("mu_g", 1),
                    ("w0", 4), ("a0", 4), ("kkb", 4), ("ka", 4), ("rk", 4), ("lnx_w", 4), ("lnx_b", 4),
                    ("lb0", 8), ("lb1", 8)):
        PK_OFF[name] = off
        off += n
    PK_N = off


_pk_layout()


def _col(v):
    return np.ascontiguousarray(np.asarray(v, np.float32).reshape(-1, 128).T)


def _consts():
    c = np.zeros((128, 256), np.float32)
    cm = np.zeros((128, 320), np.float32)
    c[:, 0:128] = np.eye(128, dtype=np.float32)
    c[0:64, 128:192] = 1.0
    c[64:128, 192:256] = 1.0
    p = np.arange(64)[:, None]; f = np.arange(64)[None, :]
    ms = (f > p).astype(np.float32)
    ml = (f < p).astype(np.float32)
    mi = (f >= p).astype(np.float32)
    idm = (f == p).astype(np.float32)
    for i, mm in enumerate((ms, ml, mi, -mi, idm)):
        cm[0:64, i * 64:(i + 1) * 64] = mm; cm[64:128, i * 64:(i + 1) * 64] = mm
    return c, cm


def _bd(w):
    o = np.zeros((128, 512), np.float32)
    for h in range(8):
        m, j = h // 2, h % 2
        o[j * 64:(j + 1) * 64, m * 128 + j * 64:m * 128 + (j + 1) * 64] = w[h]
    return o


_CACHE = {}


def kernel(**inp):
    f = lambda k: np.asarray(inp[k], np.float32)
    if "nc" not in _CACHE:
        _CACHE["nc"] = build_program()
    nc = _CACHE["nc"]
    pk = np.zeros((128, PK_N), np.float32)

    def put(name, arr):
        a = _col(arr)
        pk[:, PK_OFF[name]:PK_OFF[name] + a.shape[1]] = a
    put("ln_mix0", f("ln_mix")[0]); put("ln_mix1", f("ln_mix")[1])
    put("ln_ffn0", f("ln_ffn")[0]); put("ln_ffn1", f("ln_ffn")[1])
    put("ln_final", f("ln_final")); put("gn", f("gn_c")[0])
    cw = f("conv_w")[0]
    pk[:, PK_OFF["conv_w"]:PK_OFF["conv_w"] + 16] = np.concatenate([_col(cw[j]) for j in range(4)], axis=1)
    put("conv_b", f("conv_b")[0]); put("gr_b", f("gr_b")[0]); put("gi_b", f("gi_b")[0]); put("lam", f("lru_lambda")[0])
    mu = f("mu_b")[0]
    put("mu_r", mu[0:512]); put("mu_k", mu[512:1024]); put("mu_v", mu[1024:1536])
    put("mu_wa", mu[1536:1664]); put("mu_g", mu[1664:1792])
    put("w0", f("w0_b")[0]); put("a0", f("a0_b")[0]); put("kkb", f("kk_b")[0]); put("ka", f("ka_b")[0])
    put("rk", f("rk_b")[0].reshape(-1)); put("lnx_w", f("lnx_w")[0]); put("lnx_b", f("lnx_b")[0])
    put("lb0", f("lb_c")[0]); put("lb1", f("lb_c")[1])
    shared = {
        "w_in_ab": f("w_in_ab")[0], "w_out_ab": f("w_out_ab")[0], "w_in_c": f("w_in_c")[0], "w_out_c": f("w_out_c")[0],
        "w2": f("w2_b")[0], "a2": f("a2_b")[0], "g2": f("g2_b")[0],
        "grw_bd": _bd(f("gr_w")[0]), "giw_bd": _bd(f("gi_w")[0]), "pack128": pk, "consts": _consts()[0], "cmasks": _consts()[1],
    }
    for l in range(2):
        shared[f"ffn_gate{l}"] = f("ffn_gate")[l]; shared[f"ffn_up{l}"] = f("ffn_up")[l]; shared[f"ffn_down{l}"] = f("ffn_down")[l]
    shared = {k: np.ascontiguousarray(v) for k, v in shared.items()}
    in_maps = []
    for i in range(NCORES):
        s = slice(16 * i, 16 * i + 16)
        m = dict(shared)
        m["xp"] = np.ascontiguousarray(f("x_prompt")[i]); m["xs"] = np.ascontiguousarray(f("x_sample")[s].reshape(64, D))
        m["st_conv"] = np.ascontiguousarray(f("state_rglru_conv")[0, s]); m["st_h"] = np.ascontiguousarray(f("state_rglru_h")[0, s])
        m["st_shift"] = np.ascontiguousarray(f("state_rwkv_shift")[0, s]); m["st_rS"] = np.ascontiguousarray(f("state_rwkv_S")[0, s])
        m["st_hS"] = np.ascontiguousarray(f("state_hgrn_S")[0, s])
        in_maps.append(m)
    res = run_bass_kernel_spmd(nc, in_maps, core_ids=list(range(NCORES)))
    R = res.results
    cat = lambda k, shp: np.concatenate([np.asarray(r[k], np.float32).reshape(shp) for r in R], axis=0)[None]
    y_prompt = np.stack([np.asarray(r["yp"], np.float32) for r in R], axis=0)
    y_sample = np.concatenate([np.asarray(r["ys"], np.float32).reshape(16, 4, D) for r in R], axis=0)
    return (y_prompt, y_sample,
            cat("p_conv", (1, 3, 512)), cat("p_h", (1, 512)), cat("p_shift", (1, 1792)),
            cat("p_rS", (1, 8, 64, 64)), cat("p_hS", (1, 8, 128, 128)),
            cat("s_conv", (16, 3, 512)), cat("s_h", (16, 512)), cat("s_shift", (16, 1792)),
            cat("s_rS", (16, 8, 64, 64)), cat("s_hS", (16, 8, 128, 128)))
```
